# Optimizing a Trainium2 kernel written in Bass

```python
import math
import jax, jax.numpy as jnp
from jax import lax
import numpy as np

D_MODEL = 1024
BATCH = 8
SEQ = 4096
DEPTH = 4

N_META = 16
LEAD = 128
PAD = LEAD - N_META
EPS = 1e-6

GLA_HEADS = 4
GLA_DK = D_MODEL // 2
GLA_DV = D_MODEL
GLA_HK = GLA_DK // GLA_HEADS
GLA_HV = GLA_DV // GLA_HEADS
GLA_RANK = 16
GLA_NORMALIZER = 16.0
GLA_CHUNK = 64

SWA_HQ = 16
SWA_HKV = 2
SWA_HD = 64
SWA_GRP = SWA_HQ // SWA_HKV
SWA_WINDOW = 128
SWA_BLOCK = 128

SSD_DINNER = 2 * D_MODEL
SSD_HEADDIM = 64
SSD_HEADS = SSD_DINNER // SSD_HEADDIM
SSD_GROUPS = 4
SSD_HPG = SSD_HEADS // SSD_GROUPS
SSD_DSTATE = 128
SSD_CONV = 4
SSD_CHUNK = 128
SSD_XBC = SSD_DINNER + 2 * SSD_GROUPS * SSD_DSTATE

N_BRANCH = 3
D_FF = 4 * D_MODEL

IN_SIZES = (GLA_DK, GLA_DK, GLA_DV, GLA_DV, GLA_RANK,
            SWA_HQ * SWA_HD, SWA_HKV * SWA_HD, SWA_HKV * SWA_HD,
            SSD_DINNER, SSD_XBC, SSD_HEADS,
            N_BRANCH * D_MODEL)
N_IN = sum(IN_SIZES)

kernel_name = "hybrid_gla_swa_ssd_gated_merge"


def _rmsnorm(x, g):
    xf = x.astype(jnp.float32)
    y = xf * lax.rsqrt(jnp.mean(xf * xf, axis=-1, keepdims=True) + EPS)
    return (y * g.astype(jnp.float32)).astype(x.dtype)


def _gla(q, k, v, r, g_lr, w_gate_up, b_gate_up, norm_w):
    f32 = jnp.float32
    bsz, L, _ = q.shape
    n = L // GLA_CHUNK
    gk = jax.nn.log_sigmoid((g_lr @ w_gate_up).astype(f32) + b_gate_up.astype(f32)) / GLA_NORMALIZER

    def chunk(t, hd):
        return t.astype(f32).reshape(bsz, n, GLA_CHUNK, GLA_HEADS, hd)

    qc = chunk(q, GLA_HK) * (GLA_HK ** -0.5)
    kc = chunk(k, GLA_HK)
    vc = chunk(v, GLA_HV)
    G = jnp.cumsum(chunk(gk, GLA_HK), axis=2)
    G_last = G[:, :, -1]
    q_dec = qc * jnp.exp(G)
    k_inv = kc * jnp.exp(-G)
    causal = jnp.tril(jnp.ones((GLA_CHUNK, GLA_CHUNK), dtype=bool))
    attn = jnp.where(causal, jnp.einsum('bnihd,bnjhd->bnhij', q_dec, k_inv), 0.0)
    o_intra = jnp.einsum('bnhij,bnjhv->bnihv', attn, vc)
    k_end = kc * jnp.exp(G_last[:, :, None] - G)
    chunk_kv = jnp.einsum('bnjhd,bnjhv->bnhdv', k_end, vc)
    decay = jnp.exp(G_last)

    def step(S, inp):
        dec, kv = inp
        return S * dec[..., None] + kv, S

    S0 = jnp.zeros((bsz, GLA_HEADS, GLA_HK, GLA_HV), f32)
    _, S_prev = lax.scan(step, S0, (jnp.moveaxis(decay, 1, 0), jnp.moveaxis(chunk_kv, 1, 0)))
    S_prev = jnp.moveaxis(S_prev, 0, 1)
    o_inter = jnp.einsum('bnihd,bnhdv->bnihv', q_dec, S_prev)
    o = (o_intra + o_inter).reshape(bsz, L, GLA_HEADS, GLA_HV)
    o = o * lax.rsqrt(jnp.mean(o * o, axis=-1, keepdims=True) + EPS) * norm_w.astype(f32)
    o = o * jax.nn.silu(r.astype(f32).reshape(bsz, L, GLA_HEADS, GLA_HV))
    return o.reshape(bsz, L, GLA_DV)


def _swa(q, k, v, sinks):
    f32 = jnp.float32
    bsz, L, _ = q.shape
    nb = L // SWA_BLOCK
    qb = q.reshape(bsz, nb, SWA_BLOCK, SWA_HKV, SWA_GRP, SWA_HD)
    kb = k.reshape(bsz, nb, SWA_BLOCK, SWA_HKV, SWA_HD)
    vb = v.reshape(bsz, nb, SWA_BLOCK, SWA_HKV, SWA_HD)

    def with_prev(t):
        prev = jnp.concatenate([jnp.zeros_like(t[:, :1]), t[:, :-1]], axis=1)
        return jnp.concatenate([prev, t], axis=2)

    kw, vw = with_prev(kb), with_prev(vb)
    s = jnp.einsum('bnqhgd,bnkhd->bnhgqk', qb, kw).astype(f32) * (SWA_HD ** -0.5)
    blk = jnp.arange(nb)[:, None] * SWA_BLOCK
    qpos = blk + jnp.arange(SWA_BLOCK)[None]
    kpos = blk - SWA_BLOCK + jnp.arange(2 * SWA_BLOCK)[None]
    dq = qpos[:, :, None] - kpos[:, None, :]
    allowed = (dq >= 0) & (dq < SWA_WINDOW) & (kpos[:, None, :] >= PAD)
    s = jnp.where(allowed[None, :, None, None], s, -jnp.inf)
    sink = jnp.broadcast_to(sinks.astype(f32).reshape(SWA_HKV, SWA_GRP)[None, None, :, :, None, None],
                            s.shape[:-1] + (1,))
    p = jax.nn.softmax(jnp.concatenate([s, sink], axis=-1), axis=-1)[..., :-1]
    o = jnp.einsum('bnhgqk,bnkhd->bnqhgd', p, vw)
    return o.reshape(bsz, L, SWA_HQ * SWA_HD)


def _ssd(z, xbc, dt_raw, conv_w, conv_b, dt_bias, A_log, D_skip, norm_w, valid):
    f32 = jnp.float32
    bsz, L, _ = z.shape
    n = L // SSD_CHUNK
    xbc = lax.conv_general_dilated(xbc, conv_w[:, None, :], window_strides=(1,),
                                   padding=[(SSD_CONV - 1, 0)],
                                   dimension_numbers=('NWC', 'WIO', 'NWC'),
                                   feature_group_count=SSD_XBC) + conv_b
    xbc = jax.nn.silu(xbc.astype(f32))
    xs = xbc[..., :SSD_DINNER]
    Bm = xbc[..., SSD_DINNER:SSD_DINNER + SSD_GROUPS * SSD_DSTATE]
    Cm = xbc[..., SSD_DINNER + SSD_GROUPS * SSD_DSTATE:]
    dt = jax.nn.softplus(dt_raw.astype(f32) + dt_bias.astype(f32)) * valid[None, :, None].astype(f32)
    A = -jnp.exp(A_log.astype(f32)).reshape(SSD_GROUPS, SSD_HPG)

    xc = xs.reshape(bsz, n, SSD_CHUNK, SSD_GROUPS, SSD_HPG, SSD_HEADDIM)
    Bc = Bm.reshape(bsz, n, SSD_CHUNK, SSD_GROUPS, SSD_DSTATE)
    Cc = Cm.reshape(bsz, n, SSD_CHUNK, SSD_GROUPS, SSD_DSTATE)
    dtc = dt.reshape(bsz, n, SSD_CHUNK, SSD_GROUPS, SSD_HPG)
    a_cs = jnp.cumsum(dtc * A, axis=2)
    xdt = xc * dtc[..., None]
    causal = jnp.tril(jnp.ones((SSD_CHUNK, SSD_CHUNK), dtype=bool))
    seg = a_cs[:, :, :, None] - a_cs[:, :, None, :]
    Lmat = jnp.exp(jnp.where(causal[:, :, None, None], seg, -jnp.inf))
    CB = jnp.einsum('bnlgd,bnsgd->bnlsg', Cc, Bc)
    y_diag = jnp.einsum('bnlsgh,bnsghp->bnlghp', CB[..., None] * Lmat, xdt)
    decay_end = jnp.exp(a_cs[:, :, -1:] - a_cs)
    states = jnp.einsum('bnsgd,bnsghp->bnghpd', Bc, xdt * decay_end[..., None])
    chunk_decay = jnp.exp(a_cs[:, :, -1])

    def step(S, inp):
        dec, st = inp
        return S * dec[..., None, None] + st, S

    S0 = jnp.zeros((bsz, SSD_GROUPS, SSD_HPG, SSD_HEADDIM, SSD_DSTATE), f32)
    _, S_prev = lax.scan(step, S0, (jnp.moveaxis(chunk_decay, 1, 0), jnp.moveaxis(states, 1, 0)))
    S_prev = jnp.moveaxis(S_prev, 0, 1)
    y_off = jnp.einsum('bnlgd,bnghpd->bnlghp', Cc, S_prev) * jnp.exp(a_cs)[..., None]
    y = y_diag + y_off + D_skip.astype(f32).reshape(SSD_GROUPS, SSD_HPG)[:, :, None] * xc
    y = y.reshape(bsz, L, SSD_DINNER) * jax.nn.silu(z.astype(f32))
    y = y.reshape(bsz, L, SSD_GROUPS, SSD_DINNER // SSD_GROUPS)
    y = y * lax.rsqrt(jnp.mean(y * y, axis=-1, keepdims=True) + EPS)
    return y.reshape(bsz, L, SSD_DINNER) * norm_w.astype(f32)


def _layer(x, valid, norm_mix, w_in, b_gate, gla_w_gate_up, gla_b_gate_up, gla_norm, swa_sinks,
           ssd_conv_w, ssd_conv_b, ssd_dt_bias, ssd_A_log, ssd_D, ssd_norm,
           w_branch_gla, w_branch_swa, w_branch_ssd, w_out, norm_mlp, w_mlp_in, w_mlp_out):
    bsz, L, _ = x.shape
    vmask = valid[None, :, None]
    h = _rmsnorm(x, norm_mix)
    proj = h @ w_in
    split_pts = [int(v) for v in np.cumsum(IN_SIZES)[:-1]]
    (g_q, g_k, g_v, g_r, g_lr, s_q, s_k, s_v, m_z, m_xbc, m_dt, gate_logits) = jnp.split(proj, split_pts, axis=-1)
    y_a = _gla(g_q, g_k, g_v, g_r, g_lr, gla_w_gate_up, gla_b_gate_up, gla_norm) @ w_branch_gla
    y_b = _swa(s_q, s_k, s_v, swa_sinks) @ w_branch_swa
    y_c = _ssd(m_z, m_xbc, m_dt, ssd_conv_w, ssd_conv_b, ssd_dt_bias, ssd_A_log, ssd_D, ssd_norm, valid) @ w_branch_ssd
    gates = jax.nn.sigmoid(gate_logits.astype(jnp.float32) + b_gate.astype(jnp.float32))
    gates = gates.reshape(bsz, L, N_BRANCH, D_MODEL)
    merged = gates[:, :, 0] * y_a + gates[:, :, 1] * y_b + gates[:, :, 2] * y_c
    x = x + (merged.astype(x.dtype) @ w_out) * vmask
    h2 = _rmsnorm(x, norm_mlp)
    x = x + (jnp.square(jax.nn.relu(h2 @ w_mlp_in)) @ w_mlp_out) * vmask
    return x


def setup_inputs(seed: int = 0) -> dict:
    key = jax.random.key(seed)
    ks = iter(jax.random.split(key, 32))

    def nrm(shape, scale):
        return jax.random.normal(next(ks), shape, jnp.float32) * scale

    x = nrm((BATCH, SEQ, D_MODEL), 1.0)
    meta_tokens = nrm((N_META, D_MODEL), 1.0)
    norm_mix = 1.0 + nrm((DEPTH, D_MODEL), 0.02)
    w_in = nrm((DEPTH, D_MODEL, N_IN), D_MODEL ** -0.5)
    b_gate = nrm((DEPTH, N_BRANCH * D_MODEL), 0.1)
    gla_w_gate_up = nrm((DEPTH, GLA_RANK, GLA_DK), GLA_RANK ** -0.5)
    gla_b_gate_up = nrm((DEPTH, GLA_DK), 0.1)
    gla_norm = 1.0 + nrm((DEPTH, GLA_HV), 0.02)
    swa_sinks = nrm((DEPTH, SWA_HQ), 0.5)
    ssd_conv_w = nrm((DEPTH, SSD_CONV, SSD_XBC), SSD_CONV ** -0.5)
    ssd_conv_b = nrm((DEPTH, SSD_XBC), 0.02)
    dt0 = jnp.exp(jax.random.uniform(next(ks), (DEPTH, SSD_HEADS), jnp.float32,
                                     minval=math.log(1e-3), maxval=math.log(1e-1)))
    ssd_dt_bias = dt0 + jnp.log(-jnp.expm1(-dt0))
    ssd_A_log = jnp.log(jax.random.uniform(next(ks), (DEPTH, SSD_HEADS), jnp.float32, minval=1.0, maxval=16.0))
    ssd_D = 1.0 + nrm((DEPTH, SSD_HEADS), 0.1)
    ssd_norm = 1.0 + nrm((DEPTH, SSD_DINNER), 0.02)
    w_branch_gla = nrm((DEPTH, GLA_DV, D_MODEL), GLA_DV ** -0.5)
    w_branch_swa = nrm((DEPTH, SWA_HQ * SWA_HD, D_MODEL), (SWA_HQ * SWA_HD) ** -0.5)
    w_branch_ssd = nrm((DEPTH, SSD_DINNER, D_MODEL), SSD_DINNER ** -0.5)
    w_out = nrm((DEPTH, D_MODEL, D_MODEL), D_MODEL ** -0.5)
    norm_mlp = 1.0 + nrm((DEPTH, D_MODEL), 0.02)
    w_mlp_in = nrm((DEPTH, D_MODEL, D_FF), D_MODEL ** -0.5)
    w_mlp_out = nrm((DEPTH, D_FF, D_MODEL), D_FF ** -0.5)
    final_norm = 1.0 + nrm((D_MODEL,), 0.02)
    return {"x": x, "meta_tokens": meta_tokens, "norm_mix": norm_mix, "w_in": w_in, "b_gate": b_gate,
            "gla_w_gate_up": gla_w_gate_up, "gla_b_gate_up": gla_b_gate_up, "gla_norm": gla_norm,
            "swa_sinks": swa_sinks, "ssd_conv_w": ssd_conv_w, "ssd_conv_b": ssd_conv_b,
            "ssd_dt_bias": ssd_dt_bias, "ssd_A_log": ssd_A_log, "ssd_D": ssd_D, "ssd_norm": ssd_norm,
            "w_branch_gla": w_branch_gla, "w_branch_swa": w_branch_swa, "w_branch_ssd": w_branch_ssd,
            "w_out": w_out, "norm_mlp": norm_mlp, "w_mlp_in": w_mlp_in, "w_mlp_out": w_mlp_out,
            "final_norm": final_norm}


def reference(x, meta_tokens, norm_mix, w_in, b_gate, gla_w_gate_up, gla_b_gate_up, gla_norm, swa_sinks,
              ssd_conv_w, ssd_conv_b, ssd_dt_bias, ssd_A_log, ssd_D, ssd_norm,
              w_branch_gla, w_branch_swa, w_branch_ssd, w_out, norm_mlp, w_mlp_in, w_mlp_out, final_norm):
    bsz, seq, d = x.shape
    h = jnp.concatenate([jnp.zeros((bsz, PAD, d), x.dtype),
                         jnp.broadcast_to(meta_tokens.astype(x.dtype)[None], (bsz, N_META, d)),
                         x], axis=1)
    valid = (jnp.arange(LEAD + seq) >= PAD).astype(x.dtype)
    for i in range(DEPTH):
        h = _layer(h, valid, norm_mix[i], w_in[i], b_gate[i], gla_w_gate_up[i], gla_b_gate_up[i], gla_norm[i],
                   swa_sinks[i], ssd_conv_w[i], ssd_conv_b[i], ssd_dt_bias[i], ssd_A_log[i], ssd_D[i], ssd_norm[i],
                   w_branch_gla[i], w_branch_swa[i], w_branch_ssd[i], w_out[i], norm_mlp[i], w_mlp_in[i], w_mlp_out[i])
    h = _rmsnorm(h, final_norm)
    return h[:, LEAD:, :]
```

```python
import numpy as np
from contextlib import ExitStack
import concourse.bass as bass
import concourse.mybir as mybir
from concourse.bass_utils import run_bass_kernel_spmd

F32 = mybir.dt.float32
BF16 = mybir.dt.bfloat16
AF = mybir.ActivationFunctionType
ALU = mybir.AluOpType

P = 128
D = 1024
KC = 8
SEQ = 4096
LEAD = 128
NMETA = 16
PADN = LEAD - NMETA
LTOT = LEAD + SEQ
NTILES = LTOT // P
DEPTH = 4
EPS = 1e-6
NEG = -30000.0
QSCALE = 128.0 ** -0.5

CH_GQ, CH_GK, CH_GR, CH_SQ, CH_SKA, CH_SKB, CH_MZ, CH_XBC, CH_GATE, CH_GV, CH_SV, CH_DT, CH_GLR = (
    0, 4, 8, 16, 24, 25, 26, 42, 66, 90, 98, 99, 100)
NCH_IN = 101


def _w_in_perm():
    idx = []
    o_gq, o_gk, o_gv, o_gr, o_glr = 0, 512, 1024, 2048, 3072
    o_sq, o_sk, o_sv = 3088, 4112, 4240
    o_mz, o_xbc, o_dt, o_gate = 4368, 6416, 9488, 9520
    r = lambda a, n: list(range(a, a + n))
    idx += r(o_gq, 512) + r(o_gk, 512) + r(o_gr, 1024) + r(o_sq, 1024)
    idx += r(o_sk, 64) + r(o_sk, 64)
    idx += r(o_sk + 64, 64) + r(o_sk + 64, 64)
    idx += r(o_mz, 2048) + r(o_xbc, 3072) + r(o_gate, 3072) + r(o_gv, 1024) + r(o_sv, 128)
    idx += r(o_dt, 32) + [-1] * 96
    idx += r(o_glr, 16) + [-1] * 112
    return np.array(idx, dtype=np.int64)


PC_BGATE, PC_CONVB, PC_CONVW, PC_SINK, PC_GFIN, PC_GMIX, PC_GMLP, PC_GGLA, PC_GSSD = (
    0, 24, 48, 144, 152, 160, 168, 176, 184)
NPC = 200
CC_IDENT, CC_ONES, CC_TRIU, CC_SL, CC_MC, CC_MP, CC_MC0, CC_MP1, CC_ONA, CC_ONB, CC_VALID = (
    0, 128, 256, 384, 512, 640, 768, 896, 1024, 1152, 1280)
CC_EPS, CC_ONE, CC_LNQ = 1281, 1282, 1283
NCC = 1284


def _host_consts():
    c = np.zeros((P, NCC), np.float32)
    j = np.arange(P)[:, None]
    i = np.arange(P)[None, :]
    c[:, CC_IDENT:CC_IDENT + P] = (j == i)
    c[:, CC_ONES:CC_ONES + P] = 1.0
    c[:, CC_TRIU:CC_TRIU + P] = (j <= i)
    c[:, CC_SL:CC_SL + P] = (j > i)
    c[:, CC_MC:CC_MC + P] = np.where(j <= i, 0.0, NEG)
    c[:, CC_MP:CC_MP + P] = np.where(j > i, 0.0, NEG)
    c[:, CC_MC0:CC_MC0 + P] = np.where((j <= i) & (j >= PADN), 0.0, NEG)
    c[:, CC_MP1:CC_MP1 + P] = np.where((j > i) & (j >= PADN), 0.0, NEG)
    c[:, CC_ONA:CC_ONA + 64] = 1.0
    c[:, CC_ONB + 64:CC_ONB + 128] = 1.0
    c[PADN:, CC_VALID] = 1.0
    c[:, CC_EPS] = EPS
    c[:, CC_ONE] = 1.0
    c[:, CC_LNQ] = np.log(QSCALE)
    return c


PAGE = 2048


class Ev:
    __slots__ = ("name", "kind", "val", "snaps", "sem", "max_waited", "open_recs")

    def __init__(self, name, kind):
        self.name, self.kind, self.val = name, kind, 0
        self.snaps = {}
        self.sem = None
        self.max_waited = 0
        self.open_recs = []


class Rec:
    __slots__ = ("box", "ev", "val", "w", "pages")

    def __init__(self, box, ev, val, w):
        self.box, self.ev, self.val, self.w = box, ev, val, w
        self.pages = ()


class Root:
    def __init__(self, name, paged):
        self.name, self.paged = name, paged
        self.pages = {}
        self.recs = []

    def _pg(self, box):
        return range(box[2] // PAGE, (box[3] - 1) // PAGE + 1)

    def query(self, box):
        if not self.paged:
            return [r for r in self.recs if _ovl(r.box, box)]
        seen, out = set(), []
        for pg in self._pg(box):
            for r in self.pages.get(pg, ()):
                if id(r) not in seen and _ovl(r.box, box):
                    seen.add(id(r))
                    out.append(r)
        return out

    def add(self, rec):
        if not self.paged:
            self.recs.append(rec)
            return
        rec.pages = tuple(self._pg(rec.box))
        for pg in rec.pages:
            self.pages.setdefault(pg, []).append(rec)

    def remove(self, rec):
        if not self.paged:
            self.recs.remove(rec)
            return
        for pg in rec.pages:
            self.pages[pg].remove(rec)


def _ovl(a, b):
    return a[0] < b[1] and b[0] < a[1] and a[2] < b[3] and b[2] < a[3]


def _contains(a, b):
    return a[0] <= b[0] and b[1] <= a[1] and a[2] <= b[2] and b[3] <= a[3]


class Tile:
    def __init__(self, root, ap, p0, nparts, b0, ncols, esize):
        self.root, self.ap = root, ap
        self.p0, self.nparts, self.b0, self.ncols, self.esize = p0, nparts, b0, ncols, esize
        self.box = (p0, p0 + nparts, b0, b0 + ncols * esize)

    def __getitem__(self, key):
        rk, ck = key
        pa, pb, _ = rk.indices(self.nparts)
        ca, cb, _ = ck.indices(self.ncols)
        return Tile(self.root, self.ap[pa:pb, ca:cb], self.p0 + pa, pb - pa, self.b0 + ca * self.esize,
                    cb - ca, self.esize)

    def r3(self, a):
        return self.ap.rearrange("p (a b) -> p a b", a=a)

    def r4(self, a, b):
        return self.ap.rearrange("p (a b c) -> p a b c", a=a, b=b)


class DRegion:
    def __init__(self, root, ap, box):
        self.root, self.ap, self.box = root, ap, box


class Engine:
    def __init__(self, name, ev):
        self.name, self.ev = name, ev
        self.known = {}
        self.ops = []


class Emitter:
    def __init__(self):
        self.plan = True
        self.evs = {}
        self.eng = {}
        for n in ("pe", "act", "dve", "pool", "sp"):
            ev = self._ev("e_" + n, "eng") if n != "sp" else None
            self.eng[n] = Engine(n, ev)
        self.nops = 0

    def _ev(self, name, kind):
        if name not in self.evs:
            self.evs[name] = Ev(name, kind)
        return self.evs[name]

    def _gather(self, E, reads, writes):
        deps = {}
        known = E.known
        is_pe = E.name == "pe"
        for t in reads:
            for r in t.root.query(t.box):
                if not r.w:
                    continue
                if r.ev is E.ev and is_pe:
                    continue
                if known.get(r.ev, 0) >= r.val:
                    continue
                if deps.get(r.ev, 0) < r.val:
                    deps[r.ev] = r.val
            if t.root.name == "psum":
                b = t.box
                bb = (0, P, b[2] // 2048 * 2048, (b[3] + 2047) // 2048 * 2048)
                for r in t.root.query(bb):
                    if r.w or r.ev is E.ev:
                        continue
                    if known.get(r.ev, 0) >= r.val:
                        continue
                    if deps.get(r.ev, 0) < r.val:
                        deps[r.ev] = r.val
        for t in writes:
            for r in t.root.query(t.box):
                if r.ev is E.ev and is_pe:
                    continue
                if known.get(r.ev, 0) >= r.val:
                    continue
                if deps.get(r.ev, 0) < r.val:
                    deps[r.ev] = r.val
        return deps

    def _apply_waits(self, E, deps):
        waits = []
        items = sorted(deps.items(), key=lambda kv: kv[0].name)
        for ev, val in items:
            if E.known.get(ev, 0) >= val:
                continue
            waits.append((ev, val))
            snap = ev.snaps.get(val)
            assert snap is not None, f"missing snapshot {ev.name}@{val}"
            kn = E.known
            for k, v in snap.items():
                if kn.get(k, 0) < v:
                    kn[k] = v
            if kn.get(ev, 0) < val:
                kn[ev] = val
            if ev.kind == "dma":
                if ev.max_waited < val:
                    ev.max_waited = val
        return waits

    def _record(self, reads, writes, ev, val, dma=False):
        for t in reads:
            root = t.root
            for r in root.query(t.box):
                if (not r.w) and r.ev is ev and _contains(t.box, r.box):
                    root.remove(r)
            rec = Rec(t.box, ev, val, False)
            root.add(rec)
            if dma:
                ev.open_recs.append(rec)
        for t in writes:
            root = t.root
            for r in root.query(t.box):
                if _contains(t.box, r.box):
                    root.remove(r)
            rec = Rec(t.box, ev, val, True)
            root.add(rec)
            if dma:
                ev.open_recs.append(rec)

    def op(self, eng, fn, reads=(), writes=(), inc=True):
        self.nops += 1
        if self.plan:
            return
        E = self.eng[eng]
        deps = self._gather(E, reads, writes)
        waits = self._apply_waits(E, deps)
        if inc:
            E.ev.val += 1
            val = E.ev.val
            snap = dict(E.known)
            snap[E.ev] = val
            E.ev.snaps[val] = snap
        else:
            val = E.ev.val + 1
        self._record(reads, writes, E.ev, val)
        E.ops.append((fn, waits, (E.ev, 1) if inc else None))

    def dma(self, out_ap, in_ap, stream, reads=(), writes=(), queue="sp"):
        self.nops += 1
        if self.plan:
            return
        Q = self.eng[queue]
        s = self._ev("d_" + stream, "dma")
        deps = self._gather(Q, reads, writes)
        kn = Q.known.get(s, 0)
        if s.val > kn and s.max_waited > kn:
            deps[s] = s.val
        waits = self._apply_waits(Q, deps)
        kn = Q.known.get(s, 0)
        newval = s.val + 16
        if kn >= s.val:
            s.open_recs = []
        else:
            for r in s.open_recs:
                r.val = newval
        s.val = newval
        snap = dict(Q.known)
        prev = s.snaps.get(newval - 16)
        if prev is not None and kn < newval - 16:
            for k, v in prev.items():
                if snap.get(k, 0) < v:
                    snap[k] = v
        snap[s] = newval
        s.snaps[newval] = snap
        self._record(reads, writes, s, newval, dma=True)
        Q.ops.append((lambda e: e.dma_start(out=out_ap, in_=in_ap), waits, (s, 16)))

    def sp_barrier(self):
        if self.plan:
            return
        Q = self.eng["sp"]
        waits = []
        for ev in self.evs.values():
            if Q.known.get(ev, 0) < ev.val:
                waits.append((ev, ev.val))
        self._apply_waits(Q, dict(waits))
        Q.ops.append((None, waits, None))

    def finish(self):
        if self.plan:
            return
        Q = self.eng["sp"]
        waits = []
        for ev in self.evs.values():
            if ev.kind == "dma" and Q.known.get(ev, 0) < ev.val:
                waits.append((ev, ev.val))
        Q.ops.append((None, waits, None))

    def mm(self, out, lhsT, rhs, start=True, stop=True, out_ap=None, lhsT_ap=None, rhs_ap=None):
        oa = out.ap if out_ap is None else out_ap
        la = lhsT.ap if lhsT_ap is None else lhsT_ap
        ra = rhs.ap if rhs_ap is None else rhs_ap
        b = out.box
        wbox = DRegion(out.root, None, (0, P, b[2] // 2048 * 2048, (b[3] + 2047) // 2048 * 2048))
        self.op("pe", lambda e: e.matmul(oa, lhsT=la, rhs=ra, start=start, stop=stop),
                reads=(lhsT, rhs), writes=(wbox,), inc=stop)

    def act(self, out, in_, func, bias=None, scale=1.0, out_ap=None, in_ap=None, extra_reads=()):
        oa = out.ap if out_ap is None else out_ap
        ia = in_.ap if in_ap is None else in_ap
        reads = [in_] + list(extra_reads)
        kw = {}
        if bias is not None:
            if isinstance(bias, Tile):
                reads.append(bias)
                kw["bias"] = bias.ap
            else:
                kw["bias"] = float(bias)
        if isinstance(scale, Tile):
            reads.append(scale)
            sc = scale.ap
        else:
            sc = float(scale)
        self.op("act", lambda e: e.activation(out=oa, in_=ia, func=func, scale=sc, **kw),
                reads=reads, writes=(out,))

    def tt(self, eng, out, in0, in1, op, out_ap=None, in0_ap=None, in1_ap=None):
        oa = out.ap if out_ap is None else out_ap
        a0 = in0.ap if in0_ap is None else in0_ap
        a1 = in1.ap if in1_ap is None else in1_ap
        self.op(eng, lambda e: e.tensor_tensor(out=oa, in0=a0, in1=a1, op=op), reads=(in0, in1), writes=(out,))

    def ts(self, eng, out, in0, s1, op0, s2=None, op1=None, out_ap=None, in0_ap=None):
        oa = out.ap if out_ap is None else out_ap
        a0 = in0.ap if in0_ap is None else in0_ap
        reads = [in0]
        v1 = s1
        if isinstance(s1, Tile):
            reads.append(s1)
            v1 = s1.ap
        v2 = s2
        if isinstance(s2, Tile):
            reads.append(s2)
            v2 = s2.ap
        if op1 is None:
            self.op(eng, lambda e: e.tensor_scalar(out=oa, in0=a0, scalar1=v1, scalar2=None, op0=op0),
                    reads=reads, writes=(out,))
        else:
            self.op(eng, lambda e: e.tensor_scalar(out=oa, in0=a0, scalar1=v1, scalar2=v2, op0=op0, op1=op1),
                    reads=reads, writes=(out,))

    def stt(self, eng, out, in0, scalar, in1, op0, op1, out_ap=None, in0_ap=None, in1_ap=None):
        oa = out.ap if out_ap is None else out_ap
        a0 = in0.ap if in0_ap is None else in0_ap
        a1 = in1.ap if in1_ap is None else in1_ap
        reads = [in0, in1]
        sv = scalar
        if isinstance(scalar, Tile):
            reads.append(scalar)
            sv = scalar.ap
        self.op(eng, lambda e: e.scalar_tensor_tensor(out=oa, in0=a0, scalar=sv, in1=a1, op0=op0, op1=op1),
                reads=reads, writes=(out,))

    def copy(self, eng, out, in_, out_ap=None, in_ap=None):
        oa = out.ap if out_ap is None else out_ap
        ia = in_.ap if in_ap is None else in_ap
        if eng == "act":
            self.op("act", lambda e: e.copy(out=oa, in_=ia), reads=(in_,), writes=(out,))
        else:
            self.op(eng, lambda e: e.tensor_copy(out=oa, in_=ia), reads=(in_,), writes=(out,))

    def recip(self, out, in_, out_ap=None, in_ap=None):
        oa = out.ap if out_ap is None else out_ap
        ia = in_.ap if in_ap is None else in_ap
        self.op("dve", lambda e: e.reciprocal(out=oa, in_=ia), reads=(in_,), writes=(out,))

    def memset(self, eng, out, val):
        oa = out.ap
        self.op(eng, lambda e: e.memset(oa, val), reads=(), writes=(out,))


class Cfg:
    def __init__(self, layers, first, last, ngroups=None, debug=False):
        self.layers, self.first, self.last, self.ngroups, self.debug = layers, first, last, ngroups, debug


GROUPS = [(0, 1)] + [(1 + 3 * k, 3) for k in range(10)] + [(31, 2)]
TMAX = 384
SLOT_ELEMS = 4096
NSLOT = 4
PREPQ = "pool"
SBUF_BYTES = 212736


def build_program(cfg):
    nc = bass.Bass("TRN2", target_bir_lowering=False)
    nL = len(cfg.layers)

    def din(name, shape, dt=F32):
        return nc.dram_tensor(name, list(shape), dt, kind="ExternalInput").ap()

    if cfg.first:
        x_in = din("x", (SEQ, D))
        meta = din("meta", (P, D))
    else:
        xres_in = din("xres_in", (D, LTOT))
    consts_d = din("consts", (P, NCC))
    pcol_d = din("pcol", (nL, P, NPC))
    gp_d = din("gp", (nL, 16, 1024))
    cbrow_d = din("cbrow", (nL, 1, 2560))
    bc_d = din("bc", (nL, 1, 96))
    w_src = {"in": din("w_in", (nL, D, NCH_IN * P)), "bg": din("w_bg", (nL, D, D)), "bs": din("w_bs", (nL, D, D)),
             "bm": din("w_bm", (nL, 2 * D, D)), "out": din("w_out", (nL, D, D)), "m1": din("w_m1", (nL, D, 4 * D)),
             "m2": din("w_m2", (nL, 4 * D, D))}
    if cfg.last:
        out_d = nc.dram_tensor("out", [SEQ, D], F32, kind="ExternalOutput").ap()
    else:
        xres_out = nc.dram_tensor("xres_out", [D, LTOT], F32, kind="ExternalOutput").ap()
    dbg_d = nc.dram_tensor("dbg", [P, 8192], F32, kind="ExternalOutput").ap() if cfg.debug else None
    WS = {}
    ws_shapes = {"in": (NCH_IN, 8), "bg": (8, 8), "bs": (8, 8), "bm": (8, 16), "out": (8, 8), "m1": (32, 8),
                 "m2": (8, 32)}
    for li in range(nL):
        for nm, (nch, kc) in ws_shapes.items():
            WS[(nm, li)] = nc.dram_tensor(f"ws_{nm}_{li}", [nch, P, kc * P], BF16, kind="Internal").ap()
    xres_s = nc.dram_tensor("xres_s", [D, LTOT], F32, kind="Internal").ap() if nL > 1 else None

    K = Emitter()
    droots = {}

    def droot(name):
        if name not in droots:
            droots[name] = Root(name, False)
        return droots[name]

    with ExitStack() as es:
        arena = es.enter_context(nc.sbuf_tensor("arena", [P, SBUF_BYTES // 4], F32))
        psum = es.enter_context(nc.psum_tensor("psum", [P, 4096], F32))
        sb_root = Root("sbuf", True)
        ps_root = Root("psum", True)
        regions = [[0, SBUF_BYTES, 0]]

        def set_regions(rs):
            regions[:] = [[a, b, a] for a, b in rs]

        def alloc(ncols, dt=F32, parts=P):
            esz = 4 if dt == F32 else 2
            nbytes = (ncols * esz + 63) // 64 * 64
            for r in regions:
                if r[2] + nbytes <= r[1]:
                    b0 = r[2]
                    r[2] += nbytes
                    break
            else:
                raise AssertionError(f"SBUF overflow: need {nbytes}, regions {regions}")
            ap = arena[:, b0 // 4:(b0 + nbytes) // 4]
            if dt != F32:
                ap = ap.bitcast(dt)
            ap = ap[0:parts, 0:ncols]
            return Tile(sb_root, ap, 0, parts, b0, ncols, esz)

        ps_ptr = [0]

        ps_pool = [0, 8]

        def set_ps_pool(lo, hi):
            ps_pool[0], ps_pool[1] = lo, hi

        def PSX(b, nb):
            return Tile(ps_root, psum[:, b * 512:(b + nb) * 512], 0, P, b * 2048, nb * 512, 4)

        def PS(nb):
            if ps_ptr[0] < ps_pool[0] or ps_ptr[0] + nb > ps_pool[1]:
                ps_ptr[0] = ps_pool[0]
            b = ps_ptr[0]
            ps_ptr[0] += nb
            return PSX(b, nb)

        csm = alloc(4)
        valid0, b_eps, b_one, b_lnq = (csm[:, c:c + 1] for c in range(4))
        cb = alloc(128 * 4 + 512 * 4 + 256, BF16)
        ident_b, ones_b = cb[:, 0:128], cb[:, 128:256]
        onesA, onesB = cb[:, 256:384], cb[:, 384:512]
        mC, mP, mC0, mP1 = (cb[:, 512 + 512 * i:1024 + 512 * i] for i in range(4))
        triu_b, sl_b = cb[:, 2560:2688], cb[:, 2688:2816]
        pcol = alloc(NPC)
        pgn = alloc(40)
        gph = alloc(1024, BF16, parts=16)
        gpl = alloc(1024, BF16, parts=16)
        cbrow_b = alloc(2560, BF16, parts=1)
        bc = alloc(96)
        A_bc = alloc(32)
        esinkE = alloc(8)
        Dmat = alloc(32 * 128, BF16)
        diag = alloc(96 * 128, BF16)
        xT = alloc(KC * TMAX)
        hT = alloc(KC * TMAX, BF16)
        wslot = [alloc(SLOT_ELEMS, BF16) for _ in range(NSLOT)]
        S_f = alloc(1024)
        S_b = alloc(1024, BF16)
        ST_f = alloc(2048)
        ST_b = alloc(2048, BF16)
        kTa_h = alloc(128 + TMAX, BF16)
        kTb_h = alloc(128 + TMAX, BF16)
        vh = alloc(5 * 512, BF16)
        XS = 516
        xbcT = alloc(24 * XS, BF16)
        dbg_tmp = alloc(512) if cfg.debug else None
        U0 = regions[0][2]
        SZA = 16 * TMAX * 2
        SZB = KC * TMAX * 4
        UA = (U0, U0 + SZA)
        UB = (U0 + SZA, U0 + SZA + SZB)
        UC = (U0 + SZA + SZB, SBUF_BYTES)
        UALL = (U0, SBUF_BYTES)
        print(f"[kernel] fixed SBUF {U0} B, union {SBUF_BYTES - U0} B")
        set_regions([UB])
        macc = alloc(KC * TMAX)
        set_regions([UALL])

        wreq = []
        wstate = {"i": 0, "issued": 0}

        def wissue(j):
            nm, li, ch0, nch, kc, tm = wreq[j]
            slot = wslot[j % NSLOT]
            dst = slot[:, 0:nch * kc * P]
            reg = DRegion(droot(f"ws_{nm}_{li}"), None, (ch0, ch0 + nch, 0, 1))
            if tm:
                src = WS[(nm, li)][ch0:ch0 + nch].rearrange("j p (k x) -> p k j x", k=kc)
                K.dma(dst.ap.rearrange("p (k j x) -> p k j x", k=kc, j=nch), src, f"w{j % NSLOT}", reads=(reg,),
                      writes=(dst,))
            else:
                src = WS[(nm, li)][ch0:ch0 + nch].rearrange("j p x -> p j x")
                K.dma(dst.ap.rearrange("p (j x) -> p j x", j=nch), src, f"w{j % NSLOT}", reads=(reg,), writes=(dst,))

        def wget(nm, li, ch0, nch, kc, hold=0, tm=False):
            spec = (nm, li, ch0, nch, kc, tm)
            assert nch * kc * P <= SLOT_ELEMS
            if K.plan:
                wreq.append(spec)
                return None
            i = wstate["i"]
            assert wreq[i] == spec, (wreq[i], spec)
            lim = max(i + 1, i + NSLOT - hold)
            while wstate["issued"] < min(len(wreq), lim) and wreq[wstate["issued"]][1] <= li:
                wissue(wstate["issued"])
                wstate["issued"] += 1
            wstate["i"] += 1
            slot = wslot[i % NSLOT]
            return slot[:, 0:nch * kc * P]

        dbg_col = [0]

        def dbg_dump(tile, ncols):
            if dbg_d is None:
                return None
            c0 = dbg_col[0]
            dbg_col[0] += ncols
            assert dbg_col[0] <= 8192
            if not K.plan:
                print(f"[dbg] cols {c0}:{c0 + ncols}")
            tmp = dbg_tmp[:, 0:ncols]
            K.copy("dve", tmp[0:tile.nparts, :], tile)
            K.dma(dbg_d[0:tile.nparts, c0:c0 + ncols], tmp[0:tile.nparts, :].ap, "dbg", reads=(tmp,),
                  writes=(DRegion(droot("dbg"), None, (0, 1, c0, c0 + ncols)),))
            return c0

        def body():
            ps_ptr[0] = 0
            wstate["i"] = 0
            wstate["issued"] = 0
            dbg_col[0] = 0
            set_regions([UALL])
            cst = alloc(NCC)
            K.dma(cst.ap, consts_d, "misc", writes=(cst,))
            K.copy("dve", csm, cst[:, CC_VALID:CC_VALID + 4])
            K.copy("dve", ident_b, cst[:, CC_IDENT:CC_IDENT + P])
            K.copy("dve", ones_b, cst[:, CC_ONES:CC_ONES + P])
            K.copy("dve", onesA, cst[:, CC_ONA:CC_ONA + P])
            K.copy("dve", onesB, cst[:, CC_ONB:CC_ONB + P])
            K.copy("dve", triu_b, cst[:, CC_TRIU:CC_TRIU + P])
            K.copy("dve", sl_b, cst[:, CC_SL:CC_SL + P])
            for mt, cc in ((mC, CC_MC), (mP, CC_MP), (mC0, CC_MC0), (mP1, CC_MP1)):
                src = cst[:, cc:cc + P]
                K.copy("pool", mt, src, out_ap=mt.r3(4), in_ap=src.ap.unsqueeze(1).broadcast_to([P, 4, P]))
            for li in range(nL):
                layer(li)
            K.finish()

        def layer(li):
            first_layer = cfg.first and li == 0
            last_layer = cfg.last and li == nL - 1
            set_regions([UALL])
            cbrow_f = alloc(2560, F32, parts=1)
            gp = alloc(1024, F32, parts=16)
            K.dma(pcol.ap, pcol_d[li], "misc", writes=(pcol,))
            K.dma(gp.ap, gp_d[li], "misc", writes=(gp,))
            K.copy("dve", gph, gp)
            K.tt("dve", gpl, gp, gph, ALU.subtract)
            K.dma(cbrow_f.ap, cbrow_d[li], "misc", writes=(cbrow_f,))
            K.dma(bc.ap, bc_d[li].partition_broadcast(P), "misc", writes=(bc,))
            K.copy("dve", cbrow_b, cbrow_f)
            K.act(A_bc, bc[:, 32:64], AF.Exp)
            K.ts("dve", A_bc, A_bc, -1.0, ALU.mult)
            K.act(esinkE, pcol[:, PC_SINK:PC_SINK + 8], AF.Exp)
            K.tt("pool", Dmat, ident_b, bc[:, 64:96], ALU.mult, out_ap=Dmat.r3(32),
                 in0_ap=ident_b.ap.unsqueeze(1).broadcast_to([P, 32, P]),
                 in1_ap=bc[:, 64:96].ap.unsqueeze(2).broadcast_to([P, 32, P]))
            for c in range(24):
                for k in range(4):
                    idx = c * 4 + k
                    eng = ("dve", "pool")[idx % 2]
                    K.ts(eng, diag[:, idx * P:(idx + 1) * P], ident_b, pcol[:, PC_CONVW + idx:PC_CONVW + idx + 1],
                         ALU.mult)
            for t_ in (S_f, S_b, ST_f, ST_b, xbcT, kTa_h, kTb_h, vh):
                K.memset("pool", t_, 0.0)

            if li == 0:
                stg = [(alloc(2048), alloc(2048, BF16)) for _ in range(2)]
                for blk in prep_blocks:
                    prep_emit(0, blk, stg, pcol[:, PC_GMIX:PC_GMIX + 40])
            if li + 1 < nL:
                K.dma(pgn.ap, pcol_d[li + 1][:, PC_GMIX:PC_GMIX + 40], "misc", writes=(pgn,))
            prep_state["next"] = 0

            groups = GROUPS if cfg.ngroups is None else GROUPS[:cfg.ngroups]
            for gi, (tile0, nt) in enumerate(groups):
                group(li, gi, tile0, nt, first_layer, last_layer)

        prep_blocks = []
        for nm_ in ("in", "bg", "bs", "bm", "out", "m1", "m2"):
            nch_, kc_ = ws_shapes[nm_]
            kn_ = min(kc_, 16)
            nj_ = 16 // kn_
            for ch0_ in range(0, nch_, nj_):
                for k0_ in range(0, kc_, kn_):
                    prep_blocks.append((nm_, ch0_, min(nj_, nch_ - ch0_), k0_, kn_))
        gain_off = {"in": 0, "bg": 16, "bs": None, "bm": 24, "out": None, "m1": 8, "m2": None}
        prep_state = {"next": 0, "cnt": 0}

        def prep_emit(lw, blk, stg, gains):
            nm, ch0, n, k0, kn = blk
            kc = ws_shapes[nm][1]
            b = prep_state["cnt"] % 2
            prep_state["cnt"] += 1
            si = stg[b][0][:, 0:kn * n * P]
            so = stg[b][1][:, 0:kn * n * P]
            src = w_src[nm][lw].rearrange("(kc p) n -> p kc n", p=P)[:, k0:k0 + kn, ch0 * P:(ch0 + n) * P]
            K.dma(si.ap.rearrange("p (k x) -> p k x", k=kn), src, f"pl{b}", writes=(si,), queue=PREPQ)
            si4 = si.ap.rearrange("p (k j x) -> p k j x", k=kn, j=n)
            so4 = so.ap.rearrange("p (j k x) -> p j k x", j=n, k=kn)
            g = gain_off[nm]
            for k in range(kn):
                eng = ("act", "dve")[(prep_state["cnt"] + k) % 2]
                oa, ia = so4[:, :, k, :], si4[:, k, :, :]
                if g is None:
                    K.copy(eng, so, si, out_ap=oa, in_ap=ia)
                else:
                    gt = gains[:, g + k0 + k:g + k0 + k + 1]
                    if eng == "act":
                        K.act(so, si, AF.Copy, scale=gt, out_ap=oa, in_ap=ia)
                    else:
                        K.ts(eng, so, si, gt, ALU.mult, out_ap=oa, in0_ap=ia)
            dst = WS[(nm, lw)][ch0:ch0 + n].rearrange("j p (k x) -> p j k x", k=kc)[:, :, k0:k0 + kn, :]
            K.dma(dst, so.ap.rearrange("p (j k x) -> p j k x", j=n, k=kn), f"ps{b}", reads=(so,),
                  writes=(DRegion(droot(f"ws_{nm}_{lw}"), None, (ch0, ch0 + n, 0, 1)),), queue=PREPQ)

        def rmsnorm_to_hT(T):
            set_regions([UC])
            sq_s = alloc(KC * TMAX, BF16)
            rstd = alloc(TMAX)
            for half in range(2):
                sl = slice(half * 4 * T, (half + 1) * 4 * T)
                K.act(sq_s[:, sl], xT[:, sl], AF.Square)
            ss = PS(1)[:, 0:T]
            for c in range(KC):
                K.mm(ss, ones_b, sq_s[:, c * T:(c + 1) * T], start=(c == 0), stop=(c == KC - 1))
            K.act(rstd[:, 0:T], ss, AF.Ln, bias=b_eps, scale=1.0 / D)
            K.act(rstd[:, 0:T], rstd[:, 0:T], AF.Exp, scale=-0.5)
            for half in range(2):
                sl = slice(half * 4 * T, (half + 1) * 4 * T)
                eng = "dve"
                K.tt(eng, hT[:, sl], xT[:, sl], rstd[:, 0:T], ALU.mult,
                     out_ap=hT[:, sl].r3(4), in0_ap=xT[:, sl].r3(4),
                     in1_ap=rstd[:, 0:T].ap.unsqueeze(1).broadcast_to([P, 4, T]))
            return sq_s, rstd

        def fm_proj(li, nm, ch0, nch, kcn, rhs_tile, T, consume, per_load=4):
            per_load = min(per_load, 32 // kcn)
            j = 0
            while j < nch:
                n = min(per_load, nch - j)
                wt = wget(nm, li, ch0 + j, n, kcn)
                for jj in range(n):
                    ps = PS(1)[:, 0:T]
                    if not K.plan:
                        for k in range(kcn):
                            lw = wt[:, (jj * kcn + k) * P:(jj * kcn + k + 1) * P]
                            K.mm(ps, lw, rhs_tile[:, k * T:(k + 1) * T], start=(k == 0), stop=(k == kcn - 1))
                    consume(j + jj, ps)
                j += n

        def group(li, gi, tile0, nt, first_layer, last_layer):
            T = nt * P
            t0 = tile0 * P
            if getattr(cfg, "serialize", False):
                K.sp_barrier()
            set_regions([UC])
            if first_layer:
                if getattr(cfg, "xtm_top", False):
                    set_regions([(SBUF_BYTES - 4096, SBUF_BYTES)])
                xtm = alloc(D)
                xhi = alloc(D, BF16)
                xlo = alloc(D, BF16)
                for i in range(nt):
                    tl = tile0 + i
                    if tl == 0:
                        K.dma(xtm.ap, meta, "xl", writes=(xtm,))
                    else:
                        K.dma(xtm.ap, x_in[(tl - 1) * P:tl * P, :], "xl", writes=(xtm,))
                    K.copy("act", xhi, xtm)
                    K.tt("dve", xlo, xtm, xhi, ALU.subtract)
                    for half in range(2):
                        pt = PS(1)
                        for c4 in range(4):
                            c = half * 4 + c4
                            K.mm(pt[:, c4 * P:(c4 + 1) * P], xhi[:, c * P:(c + 1) * P], ident_b, start=True, stop=False)
                            K.mm(pt[:, c4 * P:(c4 + 1) * P], xlo[:, c * P:(c + 1) * P], ident_b, start=False, stop=True)
                        xsl = xT[:, half * 4 * T:(half + 1) * 4 * T]
                        K.copy(("dve", "act")[half], xsl, pt, out_ap=xsl.r3(4)[:, :, i * P:(i + 1) * P], in_ap=pt.r3(4))
            else:
                if li == 0:
                    sap = xres_in.rearrange("(c p) t -> p c t", p=P)[:, :, t0:t0 + T]
                    reads = ()
                else:
                    sap = xres_s.rearrange("(c p) t -> p c t", p=P)[:, :, t0:t0 + T]
                    reads = (DRegion(droot("xres"), sap, (0, 1, t0, t0 + T)),)
                K.dma(xT[:, 0:KC * T].r3(KC), sap, "xl", reads=reads, writes=(xT[:, 0:KC * T],))

            stage = getattr(cfg, "stage", 99)

            def store_xres():
                if li == nL - 1:
                    dap = xres_out
                    wr = DRegion(droot("xres_out"), None, (0, 1, t0, t0 + T))
                else:
                    dap = xres_s
                    wr = DRegion(droot("xres"), None, (0, 1, t0, t0 + T))
                dst = dap.rearrange("(c p) t -> p c t", p=P)[:, :, t0:t0 + T]
                K.dma(dst, xT[:, 0:KC * T].r3(KC), "st", reads=(xT[:, 0:KC * T],), writes=(wr,))

            if stage <= 0:
                return store_xres()
            rmsnorm_to_hT(T)
            if cfg.debug and gi == 1 and li == 0:
                dbg_dump(hT[:, 0:512], 512)
            if stage <= 1:
                return store_xres()

            ssd_phase(li, gi, tile0, nt, T)
            if stage <= 2:
                return store_xres()
            gla_phase(li, gi, tile0, nt, T)
            if stage <= 3:
                return store_xres()
            swa_phase(li, gi, tile0, nt, T)
            if stage <= 4:
                return store_xres()

            set_regions([UA, UC])
            mergedT = alloc(KC * TMAX, BF16)
            K.copy("act", mergedT[:, 0:KC * T], macc[:, 0:KC * T])
            if cfg.debug and gi == 1 and li == 0:
                dbg_dump(macc[:, 0:512], 512)

            def cons_out(j, ps):
                xs_ = xT[:, j * T:(j + 1) * T]
                K.tt("dve", xs_, ps, xs_, ALU.add)
            fm_proj(li, "out", 0, 8, 8, mergedT, T, cons_out)

            sq_s, rstd = rmsnorm_to_hT(T)
            set_regions([UA, UB, (UC[0] + KC * TMAX * 2 + TMAX * 4, UC[1])])
            uT = alloc(32 * TMAX, BF16)
            rl = [alloc(TMAX, BF16) for _ in range(2)]

            do_prep = li + 1 < nL
            if do_prep:
                stg = [(alloc(2048), alloc(2048, BF16)) for _ in range(2)]
                ngl = len(GROUPS if cfg.ngroups is None else GROUPS[:cfg.ngroups])
                quota = -(-len(prep_blocks) // ngl)
                if gi == ngl - 1:
                    quota = len(prep_blocks)
                todo = list(range(prep_state["next"], min(len(prep_blocks), prep_state["next"] + quota)))
                prep_state["next"] += len(todo)
            else:
                todo = []

            def prep_some(nb):
                for _ in range(nb):
                    if todo:
                        prep_emit(li + 1, prep_blocks[todo.pop(0)], stg, pgn)

            def cons_u(j, ps):
                r = rl[j % 2][:, 0:T]
                K.act(r, ps, AF.Relu)
                K.tt("dve", uT[:, j * T:(j + 1) * T], r, r, ALU.mult)
                if j % 4 == 3:
                    prep_some(1)
            fm_proj(li, "m1", 0, 32, 8, hT, T, cons_u)

            def cons_o(j, ps):
                xs_ = xT[:, j * T:(j + 1) * T]
                K.tt("dve", xs_, ps, xs_, ALU.add)
                prep_some(1)
            fm_proj(li, "m2", 0, 8, 32, uT, T, cons_o)
            prep_some(len(todo))
            if cfg.debug and gi == 1 and li == 0:
                dbg_dump(xT[:, 0:512], 512)

            if last_layer:
                if tile0 + nt > 1:
                    for half in range(2):
                        sl = slice(half * 4 * T, (half + 1) * 4 * T)
                        K.act(sq_s[:, sl], xT[:, sl], AF.Square)
                    ss = PS(1)[:, 0:T]
                    for c in range(KC):
                        K.mm(ss, ones_b, sq_s[:, c * T:(c + 1) * T], start=(c == 0), stop=(c == KC - 1))
                    K.act(rstd[:, 0:T], ss, AF.Ln, bias=b_eps, scale=1.0 / D)
                    K.act(rstd[:, 0:T], rstd[:, 0:T], AF.Exp, scale=-0.5)
                    for c in range(KC):
                        xs_ = xT[:, c * T:(c + 1) * T]
                        K.stt("dve", xs_, xs_, pcol[:, PC_GFIN + c:PC_GFIN + c + 1], rstd[:, 0:T],
                              ALU.mult, ALU.mult)
                    set_regions([UA, UB, (UC[0] + KC * TMAX * 2 + TMAX * 4, UC[1])])
                    otm = [alloc(D) for _ in range(2)]
                    yhi = alloc(KC * TMAX, BF16)
                    ylo = alloc(KC * TMAX, BF16)
                    K.copy("act", yhi[:, 0:KC * T], xT[:, 0:KC * T])
                    K.tt("dve", ylo[:, 0:KC * T], xT[:, 0:KC * T], yhi[:, 0:KC * T], ALU.subtract)
                    for i in range(nt):
                        tl = tile0 + i
                        if tl == 0:
                            continue
                        o = otm[i % 2]
                        for half in range(2):
                            pt = PS(1)
                            for c4 in range(4):
                                c = half * 4 + c4
                                sl_ = slice(c * T + i * P, c * T + (i + 1) * P)
                                K.mm(pt[:, c4 * P:(c4 + 1) * P], yhi[:, sl_], ident_b, start=True, stop=False)
                                K.mm(pt[:, c4 * P:(c4 + 1) * P], ylo[:, sl_], ident_b, start=False, stop=True)
                            K.copy(("dve", "act")[half], o[:, half * 512:(half + 1) * 512], pt)
                        K.dma(out_d[(tl - 1) * P:tl * P, :], o.ap, "st", reads=(o,),
                              writes=(DRegion(droot("out"), None, (0, 1, tl * P, (tl + 1) * P)),))
            else:
                store_xres()

        def branch_merge(li, nm, gate_ch0, in_tile, kcn, T, first):
            gts = [alloc(TMAX, BF16)] * 2
            tmp = [alloc(TMAX)] * 2
            per = min(4, 32 // kcn)
            for j0 in range(0, 8, per):
                wg_ = wget("in", li, CH_GATE + gate_ch0 + j0, per, 8)
                wb_ = wget(nm, li, j0, per, kcn, hold=1)
                for jj in range(per):
                    j = j0 + jj
                    g_ps = PS(1)[:, 0:T]
                    y_ps = PS(1)[:, 0:T]
                    if not K.plan:
                        for k in range(8):
                            K.mm(g_ps, wg_[:, (jj * 8 + k) * P:(jj * 8 + k + 1) * P], hT[:, k * T:(k + 1) * T],
                                 start=(k == 0), stop=(k == 7))
                        for k in range(kcn):
                            K.mm(y_ps, wb_[:, (jj * kcn + k) * P:(jj * kcn + k + 1) * P],
                                 in_tile[:, k * T:(k + 1) * T], start=(k == 0), stop=(k == kcn - 1))
                    gt = gts[j % 2][:, 0:T]
                    bcol = PC_BGATE + gate_ch0 + j
                    K.act(gt, g_ps, AF.Sigmoid, bias=pcol[:, bcol:bcol + 1])
                    mj = macc[:, j * T:(j + 1) * T]
                    if first:
                        K.tt("dve", mj, y_ps, gt, ALU.mult)
                    else:
                        tm = tmp[j % 2][:, 0:T]
                        K.tt("dve", tm, y_ps, gt, ALU.mult)
                        K.tt("dve", mj, mj, tm, ALU.add)

        def ssd_phase(li, gi, tile0, nt, T):
            set_regions([UA])
            szT = alloc(16 * TMAX, BF16)
            set_regions([(UB[0], UC[1])])
            BT = alloc(4 * TMAX, BF16)
            CT = alloc(4 * TMAX, BF16)
            xs_tm = alloc(2048, BF16)
            B_tm = alloc(512, BF16)
            sm = alloc(32 * 8)
            dtv, av, cdv, dendv, dtdv, ev_ = (sm[:, 32 * i:32 * (i + 1)] for i in range(6))
            ET = alloc(16 * P, BF16)
            lth = alloc(8 * P, BF16)
            ltl = alloc(8 * P, BF16)
            ahl = alloc(64, BF16)
            ahi, alo = ahl[:, 0:32], ahl[:, 32:64]
            eseg = alloc(8 * P, BF16)
            Mh = alloc(8 * P, BF16)
            cbm = alloc(4 * P, BF16)
            xdt = alloc(2048, BF16)
            xdtd = alloc(2048, BF16)
            yb_raw = alloc(32 * P, BF16)
            yb = Tile(sb_root, yb_raw.ap.bitcast(F32), 0, P, yb_raw.b0, 16 * P, 4)
            ysq = alloc(16 * P, BF16)
            sets = [(lth, ltl, eseg, Mh),
                    (ysq[:, 0:8 * P], ysq[:, 8 * P:16 * P], yb_raw[:, 16 * P:24 * P], yb_raw[:, 24 * P:32 * P])]
            if getattr(cfg, "noset1", False):
                sets[1] = sets[0]
            rs = alloc(4 * P)
            sz4 = szT[:, 0:16 * T].ap.rearrange("p (g c t) -> p g c t", g=4, c=4)
            sz3 = szT[:, 0:16 * T].r3(16)

            def cons_z(j, ps):
                K.act(szT[:, j * T:(j + 1) * T], ps, AF.Silu)
            fm_proj(li, "in", CH_MZ, 16, 8, hT, T, cons_z)

            def cons_xbc(j, ps):
                K.copy(("dve", "act")[j % 2], xbcT[:, j * XS + 4:j * XS + 4 + T], ps)
            fm_proj(li, "in", CH_XBC, 24, 8, hT, T, cons_xbc)

            for which, dstT in ((0, BT), (1, CT)):
                for g in range(4):
                    c = 16 + which * 4 + g
                    ps = PS(1)[:, 0:T]
                    for k in range(4):
                        K.mm(ps, diag[:, (c * 4 + k) * P:(c * 4 + k + 1) * P],
                             xbcT[:, c * XS + 1 + k:c * XS + 1 + k + T], start=(k == 0), stop=(k == 3))
                    K.act(dstT[:, g * T:(g + 1) * T], ps, AF.Silu, bias=pcol[:, PC_CONVB + c:PC_CONVB + c + 1])

            wdt = wget("in", li, CH_DT, 1, 8)
            set_ps_pool(4, 8)
            X4 = PSX(0, 4)
            for i in range(nt):
                o = i * P
                tl = tile0 + i
                sp_ = PS(1)
                dps = sp_[:, 0:32]
                if not K.plan:
                    for k in range(8):
                        K.mm(dps, hT[:, k * T + o:k * T + o + P], wdt[:, k * P:k * P + 32], start=(k == 0),
                             stop=(k == 7))
                K.tt("dve", dtv, dps, bc[:, 0:32], ALU.add)
                K.act(ev_, dtv, AF.Exp)
                K.act(dtv, ev_, AF.Ln, bias=b_one)
                if tl == 0:
                    K.ts("dve", dtv, dtv, valid0, ALU.mult)
                K.tt("dve", av, dtv, A_bc, ALU.mult)
                K.copy("dve", ahi, av)
                K.tt("dve", alo, av, ahi, ALU.subtract)
                for q in range(5):
                    cps = PS(1)
                    for c4 in range(4):
                        c = q * 4 + c4
                        oc = cps[:, c4 * P:(c4 + 1) * P]
                        for k in range(4):
                            K.mm(oc, xbcT[:, c * XS + 1 + k + o:c * XS + 1 + k + o + P],
                                 diag[:, (c * 4 + k) * P:(c * 4 + k + 1) * P], start=(k == 0), stop=False)
                        K.mm(oc, ones_b[0:1, :], cbrow_b[0:1, c * P:(c + 1) * P], start=False, stop=True)
                    if q < 4:
                        K.act(xs_tm[:, q * 512:(q + 1) * 512], cps, AF.Silu)
                    else:
                        K.act(B_tm, cps, AF.Silu)
                cbp = PS(1)
                for g in range(4):
                    K.mm(cbp[:, g * P:(g + 1) * P], BT[:, g * T + o:g * T + o + P], CT[:, g * T + o:g * T + o + P])
                K.tt("dve", cbm, cbp, triu_b, ALU.mult, out_ap=cbm.r3(4), in0_ap=cbp.r3(4),
                     in1_ap=triu_b.ap.unsqueeze(1).broadcast_to([P, 4, P]))
                sp2 = PS(1)
                K.mm(sp2[:, 32:64], ones_b, ahi, start=True, stop=False)
                K.mm(sp2[:, 32:64], ones_b, alo, start=False, stop=True)
                K.mm(sp2[:, 64:96], sl_b, ahi, start=True, stop=False)
                K.mm(sp2[:, 64:96], sl_b, alo, start=False, stop=True)
                K.act(cdv, sp2[:, 32:64], AF.Exp)
                K.act(dendv, sp2[:, 64:96], AF.Exp)
                K.tt("dve", dtdv, dtv, dendv, ALU.mult)
                acs = X4
                for h in range(32):
                    c, half = h // 2, h % 2
                    oc = acs[half * 64:(half + 1) * 64, c * P:(c + 1) * P]
                    K.mm(oc, ahi[:, h:h + 1], triu_b, start=True, stop=False,
                         lhsT_ap=ahi[:, h:h + 1].ap.broadcast_to([P, 64]))
                    K.mm(oc, alo[:, h:h + 1], triu_b, start=False, stop=True,
                         lhsT_ap=alo[:, h:h + 1].ap.broadcast_to([P, 64]))
                K.act(ET, acs, AF.Exp)
                K.tt("dve", xdt, xs_tm, dtv, ALU.mult, out_ap=xdt.r3(32), in0_ap=xs_tm.r3(32),
                     in1_ap=dtv.ap.unsqueeze(2).broadcast_to([P, 32, 64]))
                K.tt("dve", xdtd, xs_tm, dtdv, ALU.mult, out_ap=xdtd.r3(32), in0_ap=xs_tm.r3(32),
                     in1_ap=dtdv.ap.unsqueeze(2).broadcast_to([P, 32, 64]))
                A_ps = X4
                segs = {}

                def lt_ops(g):
                    lth_, ltl_ = sets[g % 2][0], sets[g % 2][1]
                    for lt_, a_ in ((lth_, ahi), (ltl_, alo)):
                        K.tt("dve", lt_, sl_b, a_[:, 8 * g:8 * g + 8], ALU.mult, out_ap=lt_.r3(8),
                             in0_ap=sl_b.ap.unsqueeze(1).broadcast_to([P, 8, P]),
                             in1_ap=a_[:, 8 * g:8 * g + 8].ap.unsqueeze(2).broadcast_to([P, 8, P]))

                def seg_mm(g):
                    lth_, ltl_ = sets[g % 2][0], sets[g % 2][1]
                    seg = PSX(4 + 2 * (g % 2), 2)
                    segs[g] = seg
                    for hh in range(8):
                        K.mm(seg[:, hh * P:(hh + 1) * P], lth_[:, hh * P:(hh + 1) * P], triu_b, start=True, stop=False)
                        K.mm(seg[:, hh * P:(hh + 1) * P], ltl_[:, hh * P:(hh + 1) * P], triu_b, start=False, stop=True)

                def exp_mh(g):
                    eseg_, Mh_ = sets[g % 2][2], sets[g % 2][3]
                    K.act(eseg_, segs[g], AF.Exp)
                    K.tt("dve", Mh_, eseg_, cbm[:, g * P:(g + 1) * P], ALU.mult, out_ap=Mh_.r3(8),
                         in0_ap=eseg_.r3(8), in1_ap=cbm[:, g * P:(g + 1) * P].ap.unsqueeze(1).broadcast_to([P, 8, P]))

                def a_mm(g):
                    Mh_ = sets[g % 2][3]
                    for hh in range(8):
                        h = 8 * g + hh
                        c, half = h // 2, h % 2
                        oc = A_ps[half * 64:(half + 1) * 64, c * P:(c + 1) * P]
                        K.mm(oc, xdt[:, h * 64:(h + 1) * 64], Mh_[:, hh * P:(hh + 1) * P], start=True, stop=False)
                        K.mm(oc, xs_tm[:, h * 64:(h + 1) * 64], Dmat[:, h * P:(h + 1) * P], start=False, stop=True)

                lt_ops(0)
                seg_mm(0)
                lt_ops(1)
                seg_mm(1)
                exp_mh(0)
                for g in range(4):
                    a_mm(g)
                    if g + 2 < 4:
                        lt_ops(g + 2)
                        seg_mm(g + 2)
                    if g + 1 < 4:
                        exp_mh(g + 1)
                Bq = PSX(4, 4)
                for c in range(16):
                    g = c // 4
                    K.mm(Bq[:, c * P:(c + 1) * P], ST_b[:, c * P:(c + 1) * P], CT[:, g * T + o:g * T + o + P])
                K.tt("dve", yb, Bq, ET, ALU.mult)
                K.tt("dve", yb, A_ps, yb, ALU.add)
                stp = X4
                for g in range(4):
                    K.mm(stp[:, g * 512:(g + 1) * 512], B_tm[:, g * P:(g + 1) * P], xdtd[:, g * 512:(g + 1) * 512])
                K.tt("dve", ST_f, ST_f, cdv, ALU.mult, out_ap=ST_f.r3(32), in0_ap=ST_f.r3(32),
                     in1_ap=cdv.ap.unsqueeze(2).broadcast_to([P, 32, 64]))
                K.tt("dve", ST_f, stp, ST_f, ALU.add)
                K.copy("act", ST_b, ST_f)
                K.tt("dve", yb, yb, szT, ALU.mult, out_ap=yb.r3(16), in0_ap=yb.r3(16), in1_ap=sz3[:, :, o:o + P])
                K.act(ysq, yb, AF.Square)
                ssp = PS(1)
                for g in range(4):
                    for c4 in range(4):
                        K.mm(ssp[:, g * P:(g + 1) * P], ones_b, ysq[:, (4 * g + c4) * P:(4 * g + c4 + 1) * P],
                             start=(c4 == 0), stop=(c4 == 3))
                K.act(rs, ssp, AF.Ln, bias=b_eps, scale=1.0 / 512)
                K.act(rs, rs, AF.Exp, scale=-0.5)
                K.tt("dve", szT, yb, rs, ALU.mult, out_ap=sz4[:, :, :, o:o + P], in0_ap=yb.r4(4, 4),
                     in1_ap=rs.r3(4).unsqueeze(2).broadcast_to([P, 4, 4, P]))
                if cfg.debug and gi == 1 and li == 0 and i == 0:
                    dbg_dump(yb[:, 0:512], 512)
                    dbg_dump(xs_tm[:, 0:512], 512)
                    dbg_dump(B_tm, 512)
                    dbg_dump(sm[:, 0:192], 192)
                    dbg_dump(ET[:, 0:512], 512)
                    dbg_dump(cbm, 512)
                    dbg_dump(Mh[:, 0:512], 512)
                    dbg_dump(CT[:, 0:128], 128)
                    dbg_dump(BT[:, 0:128], 128)
                    dbg_dump(szT[:, 0:128], 128)
            K.copy("dve", xbcT, xbcT, out_ap=xbcT.r3(24)[:, :, 1:4], in_ap=xbcT.r3(24)[:, :, T + 1:T + 4])
            set_ps_pool(0, 8)
            set_regions([UC])
            branch_merge(li, "bm", 16, szT, 16, T, True)

        def gla_phase(li, gi, tile0, nt, T):
            set_regions([UA, UC])
            glrh = alloc(TMAX, BF16, parts=16)
            glrl = alloc(TMAX, BF16, parts=16)
            sph = alloc(512, BF16)
            spl = alloc(512, BF16)
            tb = alloc(1024)
            sp = tb[:, 0:512]
            EgT = alloc(4 * TMAX)
            EiT = alloc(4 * TMAX)
            qdT = alloc(4 * TMAX, BF16)
            kiT = alloc(4 * TMAX, BF16)
            sr = alloc(8 * TMAX, BF16)
            v_all = alloc(4 * 1024, BF16)
            attnT = alloc(512, BF16)
            kitm = alloc(512, BF16)
            osq = alloc(1024, BF16)
            rs = alloc(512)
            sr3 = sr[:, 0:8 * T].r3(8)
            wl = wget("in", li, CH_GLR, 1, 8)
            gps = PS(1)[0:16, 0:T]
            if not K.plan:
                for k in range(8):
                    K.mm(gps, wl[:, k * P:k * P + 16], hT[:, k * T:(k + 1) * T], start=(k == 0), stop=(k == 7))
            K.copy("act", glrh[:, 0:T], gps)
            K.tt("dve", glrl[:, 0:T], gps, glrh[:, 0:T], ALU.subtract)
            for i in range(nt):
                o = i * P
                zps = PS(1)
                K.mm(zps, glrh[:, o:o + P], gph[0:16, 0:512], start=True, stop=False)
                K.mm(zps, glrl[:, o:o + P], gph[0:16, 0:512], start=False, stop=False)
                K.mm(zps, glrh[:, o:o + P], gpl[0:16, 0:512], start=False, stop=False)
                K.mm(zps, ones_b[0:1, :], gph[0:1, 512:1024], start=False, stop=False)
                K.mm(zps, ones_b[0:1, :], gpl[0:1, 512:1024], start=False, stop=True)
                K.act(sp, zps, AF.Exp, scale=-1.0)
                K.act(sp, sp, AF.Ln, bias=b_one)
                K.copy("act", sph, sp)
                K.tt("dve", spl, sp, sph, ALU.subtract)
                cps = PS(1)
                for h in range(4):
                    K.mm(cps[:, h * P:(h + 1) * P], sph[:, h * P:(h + 1) * P], triu_b, start=True, stop=False)
                    K.mm(cps[:, h * P:(h + 1) * P], spl[:, h * P:(h + 1) * P], triu_b, start=False, stop=True)
                K.act(EgT, cps, AF.Exp, scale=-1.0 / 16, bias=b_lnq,
                      out_ap=EgT[:, 0:4 * T].r3(4)[:, :, o:o + P], in_ap=cps.r3(4))
                K.act(EiT, cps, AF.Exp, scale=1.0 / 16,
                      out_ap=EiT[:, 0:4 * T].r3(4)[:, :, o:o + P], in_ap=cps.r3(4))

            def cons_q(j, ps):
                K.tt("dve", qdT[:, j * T:(j + 1) * T], ps, EgT[:, j * T:(j + 1) * T], ALU.mult)
            fm_proj(li, "in", CH_GQ, 4, 8, hT, T, cons_q)

            def cons_k(j, ps):
                K.tt("dve", kiT[:, j * T:(j + 1) * T], ps, EiT[:, j * T:(j + 1) * T], ALU.mult)
            fm_proj(li, "in", CH_GK, 4, 8, hT, T, cons_k)

            def cons_r(j, ps):
                K.act(sr[:, j * T:(j + 1) * T], ps, AF.Silu)
            fm_proj(li, "in", CH_GR, 8, 8, hT, T, cons_r)

            for half in range(2):
                wv = wget("in", li, CH_GV + 4 * half, 4, 8, tm=True)
                for i in range(nt):
                    o = i * P
                    vps = PS(1)
                    if not K.plan:
                        for k in range(8):
                            K.mm(vps, hT[:, k * T + o:k * T + o + P], wv[:, k * 512:(k + 1) * 512], start=(k == 0),
                                 stop=(k == 7))
                    K.copy(("act", "dve")[i % 2], v_all[:, i * 1024 + half * 512:i * 1024 + (half + 1) * 512], vps)

            for i in range(nt):
                o = i * P
                v_tm = v_all[:, i * 1024:(i + 1) * 1024]
                aps = PSX(0, 1)
                kps = PSX(1, 1)
                for h in range(4):
                    K.mm(aps[:, h * P:(h + 1) * P], kiT[:, h * T + o:h * T + o + P], qdT[:, h * T + o:h * T + o + P])
                for h in range(4):
                    K.mm(kps[:, h * P:(h + 1) * P], kiT[:, h * T + o:h * T + o + P], ident_b)
                K.tt("dve", attnT, aps, triu_b, ALU.mult, out_ap=attnT.r3(4), in0_ap=aps.r3(4),
                     in1_ap=triu_b.ap.unsqueeze(1).broadcast_to([P, 4, P]))
                K.copy("act", kitm, kps)
                kvp = PSX(4, 2)
                for h in range(4):
                    K.mm(kvp[:, h * 256:(h + 1) * 256], kitm[:, h * P:(h + 1) * P], v_tm[:, h * 256:(h + 1) * 256])
                K.tt("dve", S_f, kvp, S_f, ALU.add)
                for h in range(4):
                    dcol = EgT[:, h * T + o + P - 1:h * T + o + P]
                    K.ts("dve", S_f[:, h * 256:(h + 1) * 256], S_f[:, h * 256:(h + 1) * 256], dcol,
                         ALU.mult, 1.0 / QSCALE, ALU.mult)
                ops_ = PSX(2, 2)
                for h in range(4):
                    for vc in range(2):
                        c = 2 * h + vc
                        oc = ops_[:, c * P:(c + 1) * P]
                        K.mm(oc, v_tm[:, c * P:(c + 1) * P], attnT[:, h * P:(h + 1) * P], start=True, stop=False)
                        K.mm(oc, S_b[:, h * 256 + vc * P:h * 256 + (vc + 1) * P], qdT[:, h * T + o:h * T + o + P],
                             start=False, stop=True)
                K.copy("act", S_b, S_f)
                K.act(osq, ops_, AF.Square)
                ssp = PSX(6, 1)
                for h in range(4):
                    for vc in range(2):
                        K.mm(ssp[:, h * P:(h + 1) * P], ones_b, osq[:, (2 * h + vc) * P:(2 * h + vc + 1) * P],
                             start=(vc == 0), stop=(vc == 1))
                K.act(rs, ssp, AF.Ln, bias=b_eps, scale=1.0 / 256)
                K.act(rs, rs, AF.Exp, scale=-0.5)
                K.tt("dve", tb, ops_, rs, ALU.mult, out_ap=tb.r4(4, 2), in0_ap=ops_.r4(4, 2),
                     in1_ap=rs.r3(4).unsqueeze(2).broadcast_to([P, 4, 2, P]))
                K.tt("dve", sr, tb, sr, ALU.mult, out_ap=sr3[:, :, o:o + P], in0_ap=tb.r3(8),
                     in1_ap=sr3[:, :, o:o + P])
            branch_merge(li, "bg", 0, sr, 8, T, False)

        def swa_phase(li, gi, tile0, nt, T):
            set_regions([UA, UC])
            sqT = alloc(8 * TMAX, BF16)
            pT = [alloc(512, BF16) for _ in range(8)]
            dn = alloc(1024)
            swaT = alloc(8 * TMAX, BF16)

            sq4 = sqT[:, 0:8 * T].ap.rearrange("p (i c t) -> p i c t", i=nt, c=8)

            def cons_q(j, ps):
                K.copy(("act", "dve")[j % 2], sqT, ps, out_ap=sq4[:, :, j, :],
                       in_ap=ps.ap.rearrange("p (i t) -> p i t", i=nt))
            fm_proj(li, "in", CH_SQ, 8, 8, hT, T, cons_q)

            def cons_k(j, ps):
                dst = (kTa_h, kTb_h)[j]
                K.copy("dve", dst[:, P:P + T], ps)
            fm_proj(li, "in", CH_SKA, 2, 8, hT, T, cons_k)
            wsv = wget("in", li, CH_SV, 1, 8)
            sw3 = swaT[:, 0:8 * T].r3(8)
            set_ps_pool(0, 4)
            for i in range(nt):
                o = i * P
                tl = tile0 + i
                vps = PS(1)[:, 0:P]
                if not K.plan:
                    for k in range(8):
                        K.mm(vps, hT[:, k * T + o:k * T + o + P], wsv[:, k * P:(k + 1) * P], start=(k == 0),
                             stop=(k == 7))
                vs = vh[:, (i + 1) * 512:(i + 2) * 512]
                K.copy("dve", vs[:, 0:64], vps[:, 0:64])
                K.copy("dve", vs[:, 128 + 64:256], vps[:, 0:64])
                K.copy("dve", vs[:, 256:256 + 64], vps[:, 64:128])
                K.copy("dve", vs[:, 384 + 64:512], vps[:, 64:128])
                vp = vh[:, i * 512:(i + 1) * 512]
                kbs = ([] if tl == 0 else [0]) + [1]
                for gq in range(4):
                    kv, par = gq // 2, gq % 2
                    kh = (kTa_h, kTb_h)[kv]
                    pr = slice(par * 64, par * 64 + 64)
                    for kb in kbs:
                        sps = PS(1)
                        kcol = o + kb * P
                        qoff = i * 1024 + kv * 512
                        K.mm(sps, kh[pr, kcol:kcol + P], sqT[pr, qoff:qoff + 512], start=True, stop=False)
                        if kb == 1:
                            mk = mC0 if tl == 0 else mC
                        else:
                            mk = mP1 if tl == 1 else mP
                        K.mm(sps, ident_b, mk, start=False, stop=True)
                        K.act(pT[gq * 2 + kb], sps, AF.Exp, scale=0.125)
                ops_ = PSX(4, 2)
                dps = PSX(6, 2)
                for c in range(8):
                    kv = c // 4
                    seq = [(par, kb) for par in range(2) for kb in kbs]
                    for n, (par, kb) in enumerate(seq):
                        gq = kv * 2 + par
                        pt = pT[gq * 2 + kb][:, (c % 4) * P:(c % 4 + 1) * P]
                        vsrc = vs if kb == 1 else vp
                        vv = vsrc[:, (kv * 2 + par) * P:(kv * 2 + par + 1) * P]
                        K.mm(ops_[:, c * P:(c + 1) * P], vv, pt, start=(n == 0), stop=(n == len(seq) - 1))
                    for n, (par, kb) in enumerate(seq):
                        gq = kv * 2 + par
                        pt = pT[gq * 2 + kb][:, (c % 4) * P:(c % 4 + 1) * P]
                        K.mm(dps[:, c * P:(c + 1) * P], (onesA, onesB)[par], pt, start=(n == 0),
                             stop=(n == len(seq) - 1))
                K.tt("dve", dn, dps, esinkE, ALU.add, out_ap=dn.r3(8), in0_ap=dps.r3(8),
                     in1_ap=esinkE.ap.unsqueeze(2).broadcast_to([P, 8, P]))
                K.act(dn, dn, AF.Ln)
                K.act(dn, dn, AF.Exp, scale=-1.0)
                K.tt("dve", swaT, ops_, dn, ALU.mult, out_ap=sw3[:, :, o:o + P], in0_ap=ops_.r3(8), in1_ap=dn.r3(8))
            set_ps_pool(0, 8)
            K.copy("dve", kTa_h[:, 0:P], kTa_h[:, T:T + P])
            K.copy("dve", kTb_h[:, 0:P], kTb_h[:, T:T + P])
            K.copy("act", vh[:, 0:512], vh[:, nt * 512:(nt + 1) * 512])
            branch_merge(li, "bs", 8, swaT, 8, T, False)

        K.plan = True
        body()
        K.plan = False
        body()

        for i in range(getattr(cfg, "dummy_sems", 0)):
            es.enter_context(nc.semaphore(f"dummy{i}"))
        for ev in K.evs.values():
            ev.sem = es.enter_context(nc.semaphore(ev.name))
        print("[kernel] semaphores:", {ev.name: getattr(ev.sem, "num", ev.sem) for ev in K.evs.values()})
        block = es.enter_context(nc.Block())

        def replay(name):
            def run(e):
                for fn, waits, inc in K.eng[name].ops:
                    if fn is None:
                        for ev, val in waits:
                            e.wait_ge(ev.sem, val)
                        continue
                    attach = None
                    ws = list(waits)
                    if ws and name in ("dve", "pool") and not (inc is not None and inc[0].kind == "dma"):
                        attach = ws.pop()
                    for ev, val in ws:
                        e.wait_ge(ev.sem, val)
                    ins = fn(e)
                    if attach is not None:
                        ins._wait_ge(attach[0].sem, attach[1])
                    if inc is not None:
                        ins.then_inc(inc[0].sem, inc[1])
            return run

        block.sync(replay("sp"))
        block.tensor(replay("pe"))
        block.scalar(replay("act"))
        block.vector(replay("dve"))
        block.gpsimd(replay("pool"))
        stats = {n: len(K.eng[n].ops) for n in K.eng}
        print("[kernel] ops per engine:", stats)
    return nc, stats


def _col(v, kc):
    return np.ascontiguousarray(np.asarray(v, np.float32).reshape(kc, P).T)


def _layer_params(inp, l):
    pc = np.zeros((P, NPC), np.float32)
    pc[:, PC_BGATE:PC_BGATE + 24] = _col(inp["b_gate"][l], 24)
    pc[:, PC_CONVB:PC_CONVB + 24] = _col(inp["ssd_conv_b"][l], 24)
    cw = np.asarray(inp["ssd_conv_w"][l], np.float32)
    pc[:, PC_CONVW:PC_CONVW + 96] = cw.reshape(4, 24, P).transpose(2, 1, 0).reshape(P, 96)
    sk = np.asarray(inp["swa_sinks"][l], np.float32)
    pc[:, PC_SINK:PC_SINK + 8] = sk.reshape(8, 2)[:, (np.arange(P) >= 64).astype(np.int64)].T
    pc[:, PC_GFIN:PC_GFIN + 8] = _col(inp["final_norm"], 8)
    pc[:, PC_GMIX:PC_GMIX + 8] = _col(inp["norm_mix"][l], 8)
    pc[:, PC_GMLP:PC_GMLP + 8] = _col(inp["norm_mlp"][l], 8)
    pc[:, PC_GGLA:PC_GGLA + 8] = _col(np.tile(np.asarray(inp["gla_norm"][l], np.float32), 4), 8)
    pc[:, PC_GSSD:PC_GSSD + 16] = _col(inp["ssd_norm"][l], 16)
    gpa = np.zeros((16, 1024), np.float32)
    gpa[:, 0:512] = inp["gla_w_gate_up"][l]
    gpa[0, 512:1024] = inp["gla_b_gate_up"][l]
    cbrow = np.asarray(inp["ssd_conv_b"][l], np.float32)[None, 0:2560]
    bcr = np.concatenate([inp["ssd_dt_bias"][l], inp["ssd_A_log"][l], inp["ssd_D"][l]]).astype(np.float32)[None]
    return pc, gpa, np.ascontiguousarray(cbrow), bcr


def _lead_tile(meta_tokens):
    lead = np.zeros((P, D), np.float32)
    lead[PADN:] = np.asarray(meta_tokens, np.float32)
    return lead


_PROG_CACHE = {}


def _get_prog(key, cfg):
    if key not in _PROG_CACHE:
        _PROG_CACHE[key] = build_program(cfg)[0]
    return _PROG_CACHE[key]


def _weights_for(inp, layers):
    perm = _w_in_perm()
    w_in = np.asarray(inp["w_in"], np.float32)
    outs = []
    for l in layers:
        wl = np.zeros((D, NCH_IN * P), np.float32)
        m = perm >= 0
        wl[:, m] = w_in[l][:, perm[m]]
        outs.append(wl)
    d = {"w_in": np.stack(outs)}
    for nm, key in (("w_bg", "w_branch_gla"), ("w_bs", "w_branch_swa"), ("w_bm", "w_branch_ssd"),
                    ("w_out", "w_out"), ("w_m1", "w_mlp_in"), ("w_m2", "w_mlp_out")):
        d[nm] = np.ascontiguousarray(np.asarray(inp[key], np.float32)[list(layers)])
    ps = [_layer_params(inp, l) for l in layers]
    d["pcol"] = np.stack([p[0] for p in ps])
    d["gp"] = np.stack([p[1] for p in ps])
    d["cbrow"] = np.stack([p[2] for p in ps])
    d["bc"] = np.stack([p[3] for p in ps])
    d["consts"] = _host_consts()
    return d


FUSED = True


def kernel(**inputs):
    x = np.asarray(inputs["x"], np.float32)
    B = x.shape[0]
    meta = _lead_tile(inputs["meta_tokens"])
    if FUSED:
        cfg = Cfg(list(range(DEPTH)), True, True)
        nc = _get_prog("fused", cfg)
        shared = _weights_for(inputs, range(DEPTH))
        in_maps = [dict(shared, x=np.ascontiguousarray(x[b]), meta=meta) for b in range(B)]
        res = run_bass_kernel_spmd(nc, in_maps, core_ids=list(range(B)))
        return np.stack([r["out"] for r in res.results]).astype(np.float32)
    xres = None
    for l in range(DEPTH):
        first, last = l == 0, l == DEPTH - 1
        key = "first" if first else ("last" if last else "mid")
        nc = _get_prog(key, Cfg([l], first, last))
        shared = _weights_for(inputs, [l])
        in_maps = []
        for b in range(B):
            m = dict(shared)
            if first:
                m["x"] = np.ascontiguousarray(x[b])
                m["meta"] = meta
            else:
                m["xres_in"] = xres[b]
            in_maps.append(m)
        res = run_bass_kernel_spmd(nc, in_maps, core_ids=list(range(B)))
        if last:
            return np.stack([r["out"] for r in res.results]).astype(np.float32)
        xres = [np.ascontiguousarray(r["xres_out"]) for r in res.results]
```

```python
import numpy as np
from contextlib import ExitStack
import concourse.bass as bass
import concourse.mybir as mybir
from concourse.bass_utils import run_bass_kernel_spmd

F32 = mybir.dt.float32
BF16 = mybir.dt.bfloat16
AF = mybir.ActivationFunctionType
ALU = mybir.AluOpType

P = 128
D = 1024
KC = 8
SEQ = 4096
LEAD = 128
NMETA = 16
PADN = LEAD - NMETA
LTOT = LEAD + SEQ
NTILES = LTOT // P
DEPTH = 4
EPS = 1e-6
NEG = -30000.0
QSCALE = 128.0 ** -0.5

CH_GQ, CH_GK, CH_GR, CH_SQ, CH_SKA, CH_SKB, CH_MZ, CH_XBC, CH_GATE, CH_GV, CH_SV, CH_DT, CH_GLR = (
    0, 4, 8, 16, 24, 25, 26, 42, 66, 90, 98, 99, 100)
NCH_IN = 101


def _w_in_perm():
    idx = []
    o_gq, o_gk, o_gv, o_gr, o_glr = 0, 512, 1024, 2048, 3072
    o_sq, o_sk, o_sv = 3088, 4112, 4240
    o_mz, o_xbc, o_dt, o_gate = 4368, 6416, 9488, 9520
    r = lambda a, n: list(range(a, a + n))
    idx += r(o_gq, 512) + r(o_gk, 512) + r(o_gr, 1024) + r(o_sq, 1024)
    idx += r(o_sk, 64) + r(o_sk, 64)
    idx += r(o_sk + 64, 64) + r(o_sk + 64, 64)
    idx += r(o_mz, 2048) + r(o_xbc, 3072) + r(o_gate, 3072) + r(o_gv, 1024) + r(o_sv, 128)
    idx += r(o_dt, 32) + [-1] * 96
    idx += r(o_glr, 16) + [-1] * 112
    return np.array(idx, dtype=np.int64)


PC_BGATE, PC_CONVB, PC_CONVW, PC_SINK, PC_GFIN, PC_GMIX, PC_GMLP, PC_GGLA, PC_GSSD = (
    0, 24, 48, 144, 152, 160, 168, 176, 184)
NPC = 200
CC_IDENT, CC_ONES, CC_TRIU, CC_SL, CC_MC, CC_MP, CC_MC0, CC_MP1, CC_ONA, CC_ONB, CC_VALID = (
    0, 128, 256, 384, 512, 640, 768, 896, 1024, 1152, 1280)
CC_EPS, CC_ONE, CC_LNQ = 1281, 1282, 1283
NCC = 1284


def _host_consts():
    c = np.zeros((P, NCC), np.float32)
    j = np.arange(P)[:, None]
    i = np.arange(P)[None, :]
    c[:, CC_IDENT:CC_IDENT + P] = (j == i)
    c[:, CC_ONES:CC_ONES + P] = 1.0
    c[:, CC_TRIU:CC_TRIU + P] = (j <= i)
    c[:, CC_SL:CC_SL + P] = (j > i)
    c[:, CC_MC:CC_MC + P] = np.where(j <= i, 0.0, NEG)
    c[:, CC_MP:CC_MP + P] = np.where(j > i, 0.0, NEG)
    c[:, CC_MC0:CC_MC0 + P] = np.where((j <= i) & (j >= PADN), 0.0, NEG)
    c[:, CC_MP1:CC_MP1 + P] = np.where((j > i) & (j >= PADN), 0.0, NEG)
    c[:, CC_ONA:CC_ONA + 64] = 1.0
    c[:, CC_ONB + 64:CC_ONB + 128] = 1.0
    c[PADN:, CC_VALID] = 1.0
    c[:, CC_EPS] = EPS
    c[:, CC_ONE] = 1.0
    c[:, CC_LNQ] = np.log(QSCALE)
    return c


PAGE = 2048


class Ev:
    __slots__ = ("name", "kind", "val", "snaps", "sem", "max_waited", "open_recs")

    def __init__(self, name, kind):
        self.name, self.kind, self.val = name, kind, 0
        self.snaps = {}
        self.sem = None
        self.max_waited = 0
        self.open_recs = []


class Rec:
    __slots__ = ("box", "ev", "val", "w", "pages")

    def __init__(self, box, ev, val, w):
        self.box, self.ev, self.val, self.w = box, ev, val, w
        self.pages = ()


class Root:
    def __init__(self, name, paged):
        self.name, self.paged = name, paged
        self.pages = {}
        self.recs = []

    def _pg(self, box):
        return range(box[2] // PAGE, (box[3] - 1) // PAGE + 1)

    def query(self, box):
        if not self.paged:
            return [r for r in self.recs if _ovl(r.box, box)]
        seen, out = set(), []
        for pg in self._pg(box):
            for r in self.pages.get(pg, ()):
                if id(r) not in seen and _ovl(r.box, box):
                    seen.add(id(r))
                    out.append(r)
        return out

    def add(self, rec):
        if not self.paged:
            self.recs.append(rec)
            return
        rec.pages = tuple(self._pg(rec.box))
        for pg in rec.pages:
            self.pages.setdefault(pg, []).append(rec)

    def remove(self, rec):
        if not self.paged:
            self.recs.remove(rec)
            return
        for pg in rec.pages:
            self.pages[pg].remove(rec)


def _ovl(a, b):
    return a[0] < b[1] and b[0] < a[1] and a[2] < b[3] and b[2] < a[3]


def _contains(a, b):
    return a[0] <= b[0] and b[1] <= a[1] and a[2] <= b[2] and b[3] <= a[3]


class Tile:
    def __init__(self, root, ap, p0, nparts, b0, ncols, esize):
        self.root, self.ap = root, ap
        self.p0, self.nparts, self.b0, self.ncols, self.esize = p0, nparts, b0, ncols, esize
        self.box = (p0, p0 + nparts, b0, b0 + ncols * esize)

    def __getitem__(self, key):
        rk, ck = key
        pa, pb, _ = rk.indices(self.nparts)
        ca, cb, _ = ck.indices(self.ncols)
        return Tile(self.root, self.ap[pa:pb, ca:cb], self.p0 + pa, pb - pa, self.b0 + ca * self.esize,
                    cb - ca, self.esize)

    def r3(self, a):
        return self.ap.rearrange("p (a b) -> p a b", a=a)

    def r4(self, a, b):
        return self.ap.rearrange("p (a b c) -> p a b c", a=a, b=b)


class DRegion:
    def __init__(self, root, ap, box):
        self.root, self.ap, self.box = root, ap, box


class Engine:
    def __init__(self, name, ev):
        self.name, self.ev = name, ev
        self.known = {}
        self.ops = []


class Emitter:
    def __init__(self):
        self.plan = True
        self.evs = {}
        self.eng = {}
        for n in ("pe", "act", "dve", "pool", "sp"):
            ev = self._ev("e_" + n, "eng") if n != "sp" else None
            self.eng[n] = Engine(n, ev)
        self.nops = 0

    def _ev(self, name, kind):
        if name not in self.evs:
            self.evs[name] = Ev(name, kind)
        return self.evs[name]

    def _gather(self, E, reads, writes):
        deps = {}
        known = E.known
        is_pe = E.name == "pe"
        for t in reads:
            for r in t.root.query(t.box):
                if not r.w:
                    continue
                if r.ev is E.ev and is_pe:
                    continue
                if known.get(r.ev, 0) >= r.val:
                    continue
                if deps.get(r.ev, 0) < r.val:
                    deps[r.ev] = r.val
            if t.root.name == "psum":
                b = t.box
                bb = (0, P, b[2] // 2048 * 2048, (b[3] + 2047) // 2048 * 2048)
                for r in t.root.query(bb):
                    if r.w or r.ev is E.ev:
                        continue
                    if known.get(r.ev, 0) >= r.val:
                        continue
                    if deps.get(r.ev, 0) < r.val:
                        deps[r.ev] = r.val
        for t in writes:
            for r in t.root.query(t.box):
                if r.ev is E.ev and is_pe:
                    continue
                if known.get(r.ev, 0) >= r.val:
                    continue
                if deps.get(r.ev, 0) < r.val:
                    deps[r.ev] = r.val
        return deps

    def _apply_waits(self, E, deps):
        waits = []
        items = sorted(deps.items(), key=lambda kv: kv[0].name)
        for ev, val in items:
            if E.known.get(ev, 0) >= val:
                continue
            waits.append((ev, val))
            snap = ev.snaps.get(val)
            assert snap is not None, f"missing snapshot {ev.name}@{val}"
            kn = E.known
            for k, v in snap.items():
                if kn.get(k, 0) < v:
                    kn[k] = v
            if kn.get(ev, 0) < val:
                kn[ev] = val
            if ev.kind == "dma":
                if ev.max_waited < val:
                    ev.max_waited = val
        return waits

    def _record(self, reads, writes, ev, val, dma=False):
        for t in reads:
            root = t.root
            for r in root.query(t.box):
                if (not r.w) and r.ev is ev and _contains(t.box, r.box):
                    root.remove(r)
            rec = Rec(t.box, ev, val, False)
            root.add(rec)
            if dma:
                ev.open_recs.append(rec)
        for t in writes:
            root = t.root
            for r in root.query(t.box):
                if _contains(t.box, r.box):
                    root.remove(r)
            rec = Rec(t.box, ev, val, True)
            root.add(rec)
            if dma:
                ev.open_recs.append(rec)

    def op(self, eng, fn, reads=(), writes=(), inc=True):
        self.nops += 1
        if self.plan:
            return
        E = self.eng[eng]
        deps = self._gather(E, reads, writes)
        waits = self._apply_waits(E, deps)
        if inc:
            E.ev.val += 1
            val = E.ev.val
            snap = dict(E.known)
            snap[E.ev] = val
            E.ev.snaps[val] = snap
        else:
            val = E.ev.val + 1
        self._record(reads, writes, E.ev, val)
        E.ops.append((fn, waits, (E.ev, 1) if inc else None))

    def dma(self, out_ap, in_ap, stream, reads=(), writes=(), queue="sp"):
        self.nops += 1
        if self.plan:
            return
        Q = self.eng[queue]
        s = self._ev("d_" + stream, "dma")
        deps = self._gather(Q, reads, writes)
        kn = Q.known.get(s, 0)
        if s.val > kn and s.max_waited > kn:
            deps[s] = s.val
        waits = self._apply_waits(Q, deps)
        kn = Q.known.get(s, 0)
        newval = s.val + 16
        if kn >= s.val:
            s.open_recs = []
        else:
            for r in s.open_recs:
                r.val = newval
        s.val = newval
        snap = dict(Q.known)
        prev = s.snaps.get(newval - 16)
        if prev is not None and kn < newval - 16:
            for k, v in prev.items():
                if snap.get(k, 0) < v:
                    snap[k] = v
        snap[s] = newval
        s.snaps[newval] = snap
        self._record(reads, writes, s, newval, dma=True)
        Q.ops.append((lambda e: e.dma_start(out=out_ap, in_=in_ap), waits, (s, 16)))

    def sp_barrier(self):
        if self.plan:
            return
        Q = self.eng["sp"]
        waits = []
        for ev in self.evs.values():
            if Q.known.get(ev, 0) < ev.val:
                waits.append((ev, ev.val))
        self._apply_waits(Q, dict(waits))
        Q.ops.append((None, waits, None))

    def finish(self):
        if self.plan:
            return
        Q = self.eng["sp"]
        waits = []
        for ev in self.evs.values():
            if ev.kind == "dma" and Q.known.get(ev, 0) < ev.val:
                waits.append((ev, ev.val))
        Q.ops.append((None, waits, None))

    def mm(self, out, lhsT, rhs, start=True, stop=True, out_ap=None, lhsT_ap=None, rhs_ap=None):
        oa = out.ap if out_ap is None else out_ap
        la = lhsT.ap if lhsT_ap is None else lhsT_ap
        ra = rhs.ap if rhs_ap is None else rhs_ap
        b = out.box
        wbox = DRegion(out.root, None, (0, P, b[2] // 2048 * 2048, (b[3] + 2047) // 2048 * 2048))
        self.op("pe", lambda e: e.matmul(oa, lhsT=la, rhs=ra, start=start, stop=stop),
                reads=(lhsT, rhs), writes=(wbox,), inc=stop)

    def act(self, out, in_, func, bias=None, scale=1.0, out_ap=None, in_ap=None, extra_reads=()):
        oa = out.ap if out_ap is None else out_ap
        ia = in_.ap if in_ap is None else in_ap
        reads = [in_] + list(extra_reads)
        kw = {}
        if bias is not None:
            if isinstance(bias, Tile):
                reads.append(bias)
                kw["bias"] = bias.ap
            else:
                kw["bias"] = float(bias)
        if isinstance(scale, Tile):
            reads.append(scale)
            sc = scale.ap
        else:
            sc = float(scale)
        self.op("act", lambda e: e.activation(out=oa, in_=ia, func=func, scale=sc, **kw),
                reads=reads, writes=(out,))

    def tt(self, eng, out, in0, in1, op, out_ap=None, in0_ap=None, in1_ap=None):
        oa = out.ap if out_ap is None else out_ap
        a0 = in0.ap if in0_ap is None else in0_ap
        a1 = in1.ap if in1_ap is None else in1_ap
        self.op(eng, lambda e: e.tensor_tensor(out=oa, in0=a0, in1=a1, op=op), reads=(in0, in1), writes=(out,))

    def ts(self, eng, out, in0, s1, op0, s2=None, op1=None, out_ap=None, in0_ap=None):
        oa = out.ap if out_ap is None else out_ap
        a0 = in0.ap if in0_ap is None else in0_ap
        reads = [in0]
        v1 = s1
        if isinstance(s1, Tile):
            reads.append(s1)
            v1 = s1.ap
        v2 = s2
        if isinstance(s2, Tile):
            reads.append(s2)
            v2 = s2.ap
        if op1 is None:
            self.op(eng, lambda e: e.tensor_scalar(out=oa, in0=a0, scalar1=v1, scalar2=None, op0=op0),
                    reads=reads, writes=(out,))
        else:
            self.op(eng, lambda e: e.tensor_scalar(out=oa, in0=a0, scalar1=v1, scalar2=v2, op0=op0, op1=op1),
                    reads=reads, writes=(out,))

    def stt(self, eng, out, in0, scalar, in1, op0, op1, out_ap=None, in0_ap=None, in1_ap=None):
        oa = out.ap if out_ap is None else out_ap
        a0 = in0.ap if in0_ap is None else in0_ap
        a1 = in1.ap if in1_ap is None else in1_ap
        reads = [in0, in1]
        sv = scalar
        if isinstance(scalar, Tile):
            reads.append(scalar)
            sv = scalar.ap
        self.op(eng, lambda e: e.scalar_tensor_tensor(out=oa, in0=a0, scalar=sv, in1=a1, op0=op0, op1=op1),
                reads=reads, writes=(out,))

    def copy(self, eng, out, in_, out_ap=None, in_ap=None):
        oa = out.ap if out_ap is None else out_ap
        ia = in_.ap if in_ap is None else in_ap
        if eng == "act":
            self.op("act", lambda e: e.copy(out=oa, in_=ia), reads=(in_,), writes=(out,))
        else:
            self.op(eng, lambda e: e.tensor_copy(out=oa, in_=ia), reads=(in_,), writes=(out,))

    def recip(self, out, in_, out_ap=None, in_ap=None):
        oa = out.ap if out_ap is None else out_ap
        ia = in_.ap if in_ap is None else in_ap
        self.op("dve", lambda e: e.reciprocal(out=oa, in_=ia), reads=(in_,), writes=(out,))

    def memset(self, eng, out, val):
        oa = out.ap
        self.op(eng, lambda e: e.memset(oa, val), reads=(), writes=(out,))


class Cfg:
    def __init__(self, layers, first, last, ngroups=None, debug=False):
        self.layers, self.first, self.last, self.ngroups, self.debug = layers, first, last, ngroups, debug


GROUPS = [(0, 1)] + [(1 + 3 * k, 3) for k in range(10)] + [(31, 2)]
TMAX = 384
SLOT_ELEMS = 4096
NSLOT = 4
PREPQ = "pool"
SBUF_BYTES = 212736


def build_program(cfg):
    nc = bass.Bass("TRN2", target_bir_lowering=False)
    nL = len(cfg.layers)

    def din(name, shape, dt=F32):
        return nc.dram_tensor(name, list(shape), dt, kind="ExternalInput").ap()

    if cfg.first:
        x_in = din("x", (SEQ, D))
        meta = din("meta", (P, D))
    else:
        xres_in = din("xres_in", (D, LTOT))
    consts_d = din("consts", (P, NCC))
    pcol_d = din("pcol", (nL, P, NPC))
    gp_d = din("gp", (nL, 16, 1024))
    cbrow_d = din("cbrow", (nL, 1, 2560))
    bc_d = din("bc", (nL, 1, 96))
    w_src = {"in": din("w_in", (nL, D, NCH_IN * P)), "bg": din("w_bg", (nL, D, D)), "bs": din("w_bs", (nL, D, D)),
             "bm": din("w_bm", (nL, 2 * D, D)), "out": din("w_out", (nL, D, D)), "m1": din("w_m1", (nL, D, 4 * D)),
             "m2": din("w_m2", (nL, 4 * D, D))}
    if cfg.last:
        out_d = nc.dram_tensor("out", [SEQ, D], F32, kind="ExternalOutput").ap()
    else:
        xres_out = nc.dram_tensor("xres_out", [D, LTOT], F32, kind="ExternalOutput").ap()
    dbg_d = nc.dram_tensor("dbg", [P, 8192], F32, kind="ExternalOutput").ap() if cfg.debug else None
    WS = {}
    ws_shapes = {"in": (NCH_IN, 8), "bg": (8, 8), "bs": (8, 8), "bm": (8, 16), "out": (8, 8), "m1": (32, 8),
                 "m2": (8, 32)}
    for li in range(nL):
        for nm, (nch, kc) in ws_shapes.items():
            WS[(nm, li)] = nc.dram_tensor(f"ws_{nm}_{li}", [nch, P, kc * P], BF16, kind="Internal").ap()
    xres_s = nc.dram_tensor("xres_s", [D, LTOT], F32, kind="Internal").ap() if nL > 1 else None

    K = Emitter()
    droots = {}

    def droot(name):
        if name not in droots:
            droots[name] = Root(name, False)
        return droots[name]

    with ExitStack() as es:
        arena = es.enter_context(nc.sbuf_tensor("arena", [P, SBUF_BYTES // 4], F32))
        psum = es.enter_context(nc.psum_tensor("psum", [P, 4096], F32))
        sb_root = Root("sbuf", True)
        ps_root = Root("psum", True)
        regions = [[0, SBUF_BYTES, 0]]

        def set_regions(rs):
            regions[:] = [[a, b, a] for a, b in rs]

        def alloc(ncols, dt=F32, parts=P):
            esz = 4 if dt == F32 else 2
            nbytes = (ncols * esz + 63) // 64 * 64
            for r in regions:
                if r[2] + nbytes <= r[1]:
                    b0 = r[2]
                    r[2] += nbytes
                    break
            else:
                raise AssertionError(f"SBUF overflow: need {nbytes}, regions {regions}")
            ap = arena[:, b0 // 4:(b0 + nbytes) // 4]
            if dt != F32:
                ap = ap.bitcast(dt)
            ap = ap[0:parts, 0:ncols]
            return Tile(sb_root, ap, 0, parts, b0, ncols, esz)

        ps_ptr = [0]

        ps_pool = [0, 8]

        def set_ps_pool(lo, hi):
            ps_pool[0], ps_pool[1] = lo, hi

        def PSX(b, nb):
            return Tile(ps_root, psum[:, b * 512:(b + nb) * 512], 0, P, b * 2048, nb * 512, 4)

        def PS(nb):
            if ps_ptr[0] < ps_pool[0] or ps_ptr[0] + nb > ps_pool[1]:
                ps_ptr[0] = ps_pool[0]
            b = ps_ptr[0]
            ps_ptr[0] += nb
            return PSX(b, nb)

        csm = alloc(4)
        valid0, b_eps, b_one, b_lnq = (csm[:, c:c + 1] for c in range(4))
        cb = alloc(128 * 4 + 512 * 4 + 256, BF16)
        ident_b, ones_b = cb[:, 0:128], cb[:, 128:256]
        onesA, onesB = cb[:, 256:384], cb[:, 384:512]
        mC, mP, mC0, mP1 = (cb[:, 512 + 512 * i:1024 + 512 * i] for i in range(4))
        triu_b, sl_b = cb[:, 2560:2688], cb[:, 2688:2816]
        pcol = alloc(NPC)
        pgn = alloc(40)
        gph = alloc(1024, BF16, parts=16)
        gpl = alloc(1024, BF16, parts=16)
        cbrow_b = alloc(2560, BF16, parts=1)
        bc = alloc(96)
        A_bc = alloc(32)
        esinkE = alloc(8)
        Dmat = alloc(32 * 128, BF16)
        diag = alloc(96 * 128, BF16)
        xT = alloc(KC * TMAX)
        hT = alloc(KC * TMAX, BF16)
        wslot = [alloc(SLOT_ELEMS, BF16) for _ in range(NSLOT)]
        S_f = alloc(1024)
        S_b = alloc(1024, BF16)
        ST_f = alloc(2048)
        ST_b = alloc(2048, BF16)
        kTa_h = alloc(128 + TMAX, BF16)
        kTb_h = alloc(128 + TMAX, BF16)
        vh = alloc(5 * 512, BF16)
        XS = 516
        xbcT = alloc(24 * XS, BF16)
        dbg_tmp = alloc(512) if cfg.debug else None
        U0 = regions[0][2]
        SZA = 16 * TMAX * 2
        SZB = KC * TMAX * 4
        UA = (U0, U0 + SZA)
        UB = (U0 + SZA, U0 + SZA + SZB)
        UC = (U0 + SZA + SZB, SBUF_BYTES)
        UALL = (U0, SBUF_BYTES)
        print(f"[kernel] fixed SBUF {U0} B, union {SBUF_BYTES - U0} B")
        set_regions([UB])
        macc = alloc(KC * TMAX)
        set_regions([UALL])

        wreq = []
        wstate = {"i": 0, "issued": 0}

        def wissue(j):
            nm, li, ch0, nch, kc, tm = wreq[j]
            slot = wslot[j % NSLOT]
            dst = slot[:, 0:nch * kc * P]
            reg = DRegion(droot(f"ws_{nm}_{li}"), None, (ch0, ch0 + nch, 0, 1))
            if tm:
                src = WS[(nm, li)][ch0:ch0 + nch].rearrange("j p (k x) -> p k j x", k=kc)
                K.dma(dst.ap.rearrange("p (k j x) -> p k j x", k=kc, j=nch), src, f"w{j % NSLOT}", reads=(reg,),
                      writes=(dst,))
            else:
                src = WS[(nm, li)][ch0:ch0 + nch].rearrange("j p x -> p j x")
                K.dma(dst.ap.rearrange("p (j x) -> p j x", j=nch), src, f"w{j % NSLOT}", reads=(reg,), writes=(dst,))

        def wget(nm, li, ch0, nch, kc, hold=0, tm=False):
            spec = (nm, li, ch0, nch, kc, tm)
            assert nch * kc * P <= SLOT_ELEMS
            if K.plan:
                wreq.append(spec)
                return None
            i = wstate["i"]
            assert wreq[i] == spec, (wreq[i], spec)
            lim = max(i + 1, i + NSLOT - hold)
            while wstate["issued"] < min(len(wreq), lim) and wreq[wstate["issued"]][1] <= li:
                wissue(wstate["issued"])
                wstate["issued"] += 1
            wstate["i"] += 1
            slot = wslot[i % NSLOT]
            return slot[:, 0:nch * kc * P]

        dbg_col = [0]

        def dbg_dump(tile, ncols):
            if dbg_d is None:
                return None
            c0 = dbg_col[0]
            dbg_col[0] += ncols
            assert dbg_col[0] <= 8192
            if not K.plan:
                print(f"[dbg] cols {c0}:{c0 + ncols}")
            tmp = dbg_tmp[:, 0:ncols]
            K.copy("dve", tmp[0:tile.nparts, :], tile)
            K.dma(dbg_d[0:tile.nparts, c0:c0 + ncols], tmp[0:tile.nparts, :].ap, "dbg", reads=(tmp,),
                  writes=(DRegion(droot("dbg"), None, (0, 1, c0, c0 + ncols)),))
            return c0

        preloaded = {}

        def body():
            preloaded.clear()
            ps_ptr[0] = 0
            wstate["i"] = 0
            wstate["issued"] = 0
            dbg_col[0] = 0
            set_regions([UALL])
            cst = alloc(NCC)
            K.dma(cst.ap, consts_d, "misc", writes=(cst,))
            K.copy("dve", csm, cst[:, CC_VALID:CC_VALID + 4])
            K.copy("dve", ident_b, cst[:, CC_IDENT:CC_IDENT + P])
            K.copy("dve", ones_b, cst[:, CC_ONES:CC_ONES + P])
            K.copy("dve", onesA, cst[:, CC_ONA:CC_ONA + P])
            K.copy("dve", onesB, cst[:, CC_ONB:CC_ONB + P])
            K.copy("dve", triu_b, cst[:, CC_TRIU:CC_TRIU + P])
            K.copy("dve", sl_b, cst[:, CC_SL:CC_SL + P])
            for mt, cc in ((mC, CC_MC), (mP, CC_MP), (mC0, CC_MC0), (mP1, CC_MP1)):
                src = cst[:, cc:cc + P]
                K.copy("pool", mt, src, out_ap=mt.r3(4), in_ap=src.ap.unsqueeze(1).broadcast_to([P, 4, P]))
            for li in range(nL):
                layer(li)
            K.finish()

        def layer(li):
            first_layer = cfg.first and li == 0
            last_layer = cfg.last and li == nL - 1
            set_regions([UALL])
            cbrow_f = alloc(2560, F32, parts=1)
            gp = alloc(1024, F32, parts=16)
            K.dma(pcol.ap, pcol_d[li], "misc", writes=(pcol,))
            K.dma(gp.ap, gp_d[li], "misc", writes=(gp,))
            K.copy("dve", gph, gp)
            K.tt("dve", gpl, gp, gph, ALU.subtract)
            K.dma(cbrow_f.ap, cbrow_d[li], "misc", writes=(cbrow_f,))
            K.dma(bc.ap, bc_d[li].partition_broadcast(P), "misc", writes=(bc,))
            K.copy("dve", cbrow_b, cbrow_f)
            K.act(A_bc, bc[:, 32:64], AF.Exp)
            K.ts("dve", A_bc, A_bc, -1.0, ALU.mult)
            K.act(esinkE, pcol[:, PC_SINK:PC_SINK + 8], AF.Exp)
            K.tt("pool", Dmat, ident_b, bc[:, 64:96], ALU.mult, out_ap=Dmat.r3(32),
                 in0_ap=ident_b.ap.unsqueeze(1).broadcast_to([P, 32, P]),
                 in1_ap=bc[:, 64:96].ap.unsqueeze(2).broadcast_to([P, 32, P]))
            for c in range(24):
                for k in range(4):
                    idx = c * 4 + k
                    eng = ("dve", "pool")[idx % 2]
                    K.ts(eng, diag[:, idx * P:(idx + 1) * P], ident_b, pcol[:, PC_CONVW + idx:PC_CONVW + idx + 1],
                         ALU.mult)
            for t_ in (S_f, S_b, ST_f, ST_b, xbcT, kTa_h, kTb_h, vh):
                K.memset("pool", t_, 0.0)

            if li == 0:
                stg = [(alloc(2048), alloc(2048, BF16)) for _ in range(2)]
                for blk in prep_blocks:
                    prep_emit(0, blk, stg, pcol[:, PC_GMIX:PC_GMIX + 40])
            if li + 1 < nL:
                K.dma(pgn.ap, pcol_d[li + 1][:, PC_GMIX:PC_GMIX + 40], "misc", writes=(pgn,))
            prep_state["next"] = 0

            groups = GROUPS if cfg.ngroups is None else GROUPS[:cfg.ngroups]
            for gi, (tile0, nt) in enumerate(groups):
                group(li, gi, tile0, nt, first_layer, last_layer)

        prep_blocks = []
        for nm_ in ("in", "bg", "bs", "bm", "out", "m1", "m2"):
            nch_, kc_ = ws_shapes[nm_]
            kn_ = min(kc_, 16)
            nj_ = 16 // kn_
            for ch0_ in range(0, nch_, nj_):
                for k0_ in range(0, kc_, kn_):
                    prep_blocks.append((nm_, ch0_, min(nj_, nch_ - ch0_), k0_, kn_))
        gain_off = {"in": 0, "bg": 16, "bs": None, "bm": 24, "out": None, "m1": 8, "m2": None}
        prep_state = {"next": 0, "cnt": 0}

        def prep_emit(lw, blk, stg, gains):
            nm, ch0, n, k0, kn = blk
            kc = ws_shapes[nm][1]
            b = prep_state["cnt"] % 2
            prep_state["cnt"] += 1
            si = stg[b][0][:, 0:kn * n * P]
            so = stg[b][1][:, 0:kn * n * P]
            src = w_src[nm][lw].rearrange("(kc p) n -> p kc n", p=P)[:, k0:k0 + kn, ch0 * P:(ch0 + n) * P]
            K.dma(si.ap.rearrange("p (k x) -> p k x", k=kn), src, f"pl{b}", writes=(si,), queue=PREPQ)
            si4 = si.ap.rearrange("p (k j x) -> p k j x", k=kn, j=n)
            so4 = so.ap.rearrange("p (j k x) -> p j k x", j=n, k=kn)
            g = gain_off[nm]
            for k in range(kn):
                eng = ("act", "dve")[(prep_state["cnt"] + k) % 2]
                oa, ia = so4[:, :, k, :], si4[:, k, :, :]
                if g is None:
                    K.copy(eng, so, si, out_ap=oa, in_ap=ia)
                else:
                    gt = gains[:, g + k0 + k:g + k0 + k + 1]
                    if eng == "act":
                        K.act(so, si, AF.Copy, scale=gt, out_ap=oa, in_ap=ia)
                    else:
                        K.ts(eng, so, si, gt, ALU.mult, out_ap=oa, in0_ap=ia)
            dst = WS[(nm, lw)][ch0:ch0 + n].rearrange("j p (k x) -> p j k x", k=kc)[:, :, k0:k0 + kn, :]
            K.dma(dst, so.ap.rearrange("p (j k x) -> p j k x", j=n, k=kn), f"ps{b}", reads=(so,),
                  writes=(DRegion(droot(f"ws_{nm}_{lw}"), None, (ch0, ch0 + n, 0, 1)),), queue=PREPQ)

        def rmsnorm_to_hT(T):
            set_regions([UC])
            sq_s = alloc(KC * TMAX, BF16)
            rstd = alloc(TMAX)
            for half in range(2):
                sl = slice(half * 4 * T, (half + 1) * 4 * T)
                K.act(sq_s[:, sl], xT[:, sl], AF.Square)
            ss = PS(1)[:, 0:T]
            for c in range(KC):
                K.mm(ss, ones_b, sq_s[:, c * T:(c + 1) * T], start=(c == 0), stop=(c == KC - 1))
            K.act(rstd[:, 0:T], ss, AF.Ln, bias=b_eps, scale=1.0 / D)
            K.act(rstd[:, 0:T], rstd[:, 0:T], AF.Exp, scale=-0.5)
            for half in range(2):
                sl = slice(half * 4 * T, (half + 1) * 4 * T)
                eng = "dve"
                K.tt(eng, hT[:, sl], xT[:, sl], rstd[:, 0:T], ALU.mult,
                     out_ap=hT[:, sl].r3(4), in0_ap=xT[:, sl].r3(4),
                     in1_ap=rstd[:, 0:T].ap.unsqueeze(1).broadcast_to([P, 4, T]))
            return sq_s, rstd

        def fm_proj(li, nm, ch0, nch, kcn, rhs_tile, T, consume, per_load=4):
            per_load = min(per_load, 32 // kcn)
            j = 0
            while j < nch:
                n = min(per_load, nch - j)
                wt = wget(nm, li, ch0 + j, n, kcn)
                for jj in range(n):
                    ps = PS(1)[:, 0:T]
                    if not K.plan:
                        for k in range(kcn):
                            lw = wt[:, (jj * kcn + k) * P:(jj * kcn + k + 1) * P]
                            K.mm(ps, lw, rhs_tile[:, k * T:(k + 1) * T], start=(k == 0), stop=(k == kcn - 1))
                    consume(j + jj, ps)
                j += n

        def group(li, gi, tile0, nt, first_layer, last_layer):
            T = nt * P
            t0 = tile0 * P
            if getattr(cfg, "serialize", False):
                K.sp_barrier()
            set_regions([UC])
            if first_layer:
                if getattr(cfg, "xtm_top", False):
                    set_regions([(SBUF_BYTES - 4096, SBUF_BYTES)])
                xtm = alloc(D)
                xhi = alloc(D, BF16)
                xlo = alloc(D, BF16)
                for i in range(nt):
                    tl = tile0 + i
                    if tl == 0:
                        K.dma(xtm.ap, meta, "xl", writes=(xtm,))
                    else:
                        K.dma(xtm.ap, x_in[(tl - 1) * P:tl * P, :], "xl", writes=(xtm,))
                    K.copy("act", xhi, xtm)
                    K.tt("dve", xlo, xtm, xhi, ALU.subtract)
                    for half in range(2):
                        pt = PS(1)
                        for c4 in range(4):
                            c = half * 4 + c4
                            K.mm(pt[:, c4 * P:(c4 + 1) * P], xhi[:, c * P:(c + 1) * P], ident_b, start=True, stop=False)
                            K.mm(pt[:, c4 * P:(c4 + 1) * P], xlo[:, c * P:(c + 1) * P], ident_b, start=False, stop=True)
                        xsl = xT[:, half * 4 * T:(half + 1) * 4 * T]
                        K.copy(("dve", "act")[half], xsl, pt, out_ap=xsl.r3(4)[:, :, i * P:(i + 1) * P], in_ap=pt.r3(4))
            elif preloaded.get((li, gi), False):
                pass
            else:
                if li == 0:
                    sap = xres_in.rearrange("(c p) t -> p c t", p=P)[:, :, t0:t0 + T]
                    reads = ()
                else:
                    sap = xres_s.rearrange("(c p) t -> p c t", p=P)[:, :, t0:t0 + T]
                    reads = (DRegion(droot("xres"), sap, (0, KC, t0, t0 + T)),)
                K.dma(xT[:, 0:KC * T].r3(KC), sap, "xl", reads=reads, writes=(xT[:, 0:KC * T],))

            stage = getattr(cfg, "stage", 99)

            def store_xres():
                if li == nL - 1:
                    dap = xres_out
                    wr = DRegion(droot("xres_out"), None, (0, 1, t0, t0 + T))
                else:
                    dap = xres_s
                    wr = DRegion(droot("xres"), None, (0, KC, t0, t0 + T))
                dst = dap.rearrange("(c p) t -> p c t", p=P)[:, :, t0:t0 + T]
                K.dma(dst, xT[:, 0:KC * T].r3(KC), "st", reads=(xT[:, 0:KC * T],), writes=(wr,))

            if stage <= 0:
                return store_xres()
            rmsnorm_to_hT(T)
            if cfg.debug and gi == 1 and li == 0:
                dbg_dump(hT[:, 0:512], 512)
            if stage <= 1:
                return store_xres()

            ssd_phase(li, gi, tile0, nt, T)
            if stage <= 2:
                return store_xres()
            gla_phase(li, gi, tile0, nt, T)
            if stage <= 3:
                return store_xres()
            swa_phase(li, gi, tile0, nt, T)
            if stage <= 4:
                return store_xres()

            set_regions([UA, UC])
            mergedT = alloc(KC * TMAX, BF16)
            K.copy("act", mergedT[:, 0:KC * T], macc[:, 0:KC * T])
            if cfg.debug and gi == 1 and li == 0:
                dbg_dump(macc[:, 0:512], 512)

            def cons_out(j, ps):
                xs_ = xT[:, j * T:(j + 1) * T]
                K.tt("dve", xs_, ps, xs_, ALU.add)
            fm_proj(li, "out", 0, 8, 8, mergedT, T, cons_out)

            sq_s, rstd = rmsnorm_to_hT(T)
            set_regions([UA, UB, (UC[0] + KC * TMAX * 2 + TMAX * 4, UC[1])])
            uT = alloc(32 * TMAX, BF16)
            rl = [alloc(TMAX, BF16) for _ in range(2)]

            do_prep = li + 1 < nL
            if do_prep:
                stg = [(alloc(2048), alloc(2048, BF16)) for _ in range(2)]
                ngl = len(GROUPS if cfg.ngroups is None else GROUPS[:cfg.ngroups])
                quota = -(-len(prep_blocks) // ngl)
                if gi == ngl - 1:
                    quota = len(prep_blocks)
                todo = list(range(prep_state["next"], min(len(prep_blocks), prep_state["next"] + quota)))
                prep_state["next"] += len(todo)
            else:
                todo = []

            def prep_some(nb):
                for _ in range(nb):
                    if todo:
                        prep_emit(li + 1, prep_blocks[todo.pop(0)], stg, pgn)

            def cons_u(j, ps):
                r = rl[j % 2][:, 0:T]
                K.act(r, ps, AF.Relu)
                K.tt("dve", uT[:, j * T:(j + 1) * T], r, r, ALU.mult)
                if j % 4 == 3:
                    prep_some(1)
            fm_proj(li, "m1", 0, 32, 8, hT, T, cons_u)

            groups_l = GROUPS if cfg.ngroups is None else GROUPS[:cfg.ngroups]
            chunk_store = (not last_layer) and li < nL - 1 and stage >= 99
            nxt = groups_l[gi + 1] if gi + 1 < len(groups_l) else None
            chunk_pre = chunk_store and li > 0 and nxt is not None and nxt[1] * P == T
            if chunk_pre:
                preloaded[(li, gi + 1)] = True

            def cons_o(j, ps):
                xs_ = xT[:, j * T:(j + 1) * T]
                K.tt("dve", xs_, ps, xs_, ALU.add)
                if chunk_store:
                    dst = xres_s.rearrange("(c p) t -> p c t", p=P)[:, j, t0:t0 + T]
                    K.dma(dst, xs_.ap, "stc", reads=(xs_,),
                          writes=(DRegion(droot("xres"), None, (j, j + 1, t0, t0 + T)),), queue=PREPQ)
                    if chunk_pre:
                        tn = nxt[0] * P
                        src = xres_s.rearrange("(c p) t -> p c t", p=P)[:, j, tn:tn + T]
                        K.dma(xs_.ap, src, "xlc", reads=(DRegion(droot("xres"), None, (j, j + 1, tn, tn + T)),),
                              writes=(xs_,), queue=PREPQ)
                prep_some(1)
            fm_proj(li, "m2", 0, 8, 32, uT, T, cons_o)
            prep_some(len(todo))
            if cfg.debug and gi == 1 and li == 0:
                dbg_dump(xT[:, 0:512], 512)

            if last_layer:
                if tile0 + nt > 1:
                    for half in range(2):
                        sl = slice(half * 4 * T, (half + 1) * 4 * T)
                        K.act(sq_s[:, sl], xT[:, sl], AF.Square)
                    ss = PS(1)[:, 0:T]
                    for c in range(KC):
                        K.mm(ss, ones_b, sq_s[:, c * T:(c + 1) * T], start=(c == 0), stop=(c == KC - 1))
                    K.act(rstd[:, 0:T], ss, AF.Ln, bias=b_eps, scale=1.0 / D)
                    K.act(rstd[:, 0:T], rstd[:, 0:T], AF.Exp, scale=-0.5)
                    for c in range(KC):
                        xs_ = xT[:, c * T:(c + 1) * T]
                        K.stt("dve", xs_, xs_, pcol[:, PC_GFIN + c:PC_GFIN + c + 1], rstd[:, 0:T],
                              ALU.mult, ALU.mult)
                    set_regions([UA, UB, (UC[0] + KC * TMAX * 2 + TMAX * 4, UC[1])])
                    otm = [alloc(D) for _ in range(2)]
                    yhi = alloc(KC * TMAX, BF16)
                    ylo = alloc(KC * TMAX, BF16)
                    K.copy("act", yhi[:, 0:KC * T], xT[:, 0:KC * T])
                    K.tt("dve", ylo[:, 0:KC * T], xT[:, 0:KC * T], yhi[:, 0:KC * T], ALU.subtract)
                    for i in range(nt):
                        tl = tile0 + i
                        if tl == 0:
                            continue
                        o = otm[i % 2]
                        for half in range(2):
                            pt = PS(1)
                            for c4 in range(4):
                                c = half * 4 + c4
                                sl_ = slice(c * T + i * P, c * T + (i + 1) * P)
                                K.mm(pt[:, c4 * P:(c4 + 1) * P], yhi[:, sl_], ident_b, start=True, stop=False)
                                K.mm(pt[:, c4 * P:(c4 + 1) * P], ylo[:, sl_], ident_b, start=False, stop=True)
                            K.copy(("dve", "act")[half], o[:, half * 512:(half + 1) * 512], pt)
                        K.dma(out_d[(tl - 1) * P:tl * P, :], o.ap, "st", reads=(o,),
                              writes=(DRegion(droot("out"), None, (0, 1, tl * P, (tl + 1) * P)),))
            elif not chunk_store:
                store_xres()

        def branch_merge(li, nm, gate_ch0, in_tile, kcn, T, first):
            gts = [alloc(TMAX, BF16)] * 2
            tmp = [alloc(TMAX)] * 2
            per = min(4, 32 // kcn)
            for j0 in range(0, 8, per):
                wg_ = wget("in", li, CH_GATE + gate_ch0 + j0, per, 8)
                wb_ = wget(nm, li, j0, per, kcn, hold=1)
                for jj in range(per):
                    j = j0 + jj
                    g_ps = PS(1)[:, 0:T]
                    y_ps = PS(1)[:, 0:T]
                    if not K.plan:
                        for k in range(8):
                            K.mm(g_ps, wg_[:, (jj * 8 + k) * P:(jj * 8 + k + 1) * P], hT[:, k * T:(k + 1) * T],
                                 start=(k == 0), stop=(k == 7))
                        for k in range(kcn):
                            K.mm(y_ps, wb_[:, (jj * kcn + k) * P:(jj * kcn + k + 1) * P],
                                 in_tile[:, k * T:(k + 1) * T], start=(k == 0), stop=(k == kcn - 1))
                    gt = gts[j % 2][:, 0:T]
                    bcol = PC_BGATE + gate_ch0 + j
                    K.act(gt, g_ps, AF.Sigmoid, bias=pcol[:, bcol:bcol + 1])
                    mj = macc[:, j * T:(j + 1) * T]
                    if first:
                        K.tt("dve", mj, y_ps, gt, ALU.mult)
                    else:
                        tm = tmp[j % 2][:, 0:T]
                        K.tt("dve", tm, y_ps, gt, ALU.mult)
                        K.tt("dve", mj, mj, tm, ALU.add)

        def ssd_phase(li, gi, tile0, nt, T):
            set_regions([UA])
            szT = alloc(16 * TMAX, BF16)
            set_regions([(UB[0], UC[1])])
            BT = alloc(4 * TMAX, BF16)
            CT = alloc(4 * TMAX, BF16)
            xs_tm = alloc(2048, BF16)
            B_tm = alloc(512, BF16)
            sm = alloc(32 * 8)
            dtv, av, cdv, dendv, dtdv, ev_ = (sm[:, 32 * i:32 * (i + 1)] for i in range(6))
            ET = alloc(16 * P, BF16)
            lth = alloc(8 * P, BF16)
            ltl = alloc(8 * P, BF16)
            ahl = alloc(64, BF16)
            ahi, alo = ahl[:, 0:32], ahl[:, 32:64]
            eseg = alloc(8 * P, BF16)
            Mh = alloc(8 * P, BF16)
            cbm = alloc(4 * P, BF16)
            xdt = alloc(2048, BF16)
            xdtd = alloc(2048, BF16)
            yb_raw = alloc(32 * P, BF16)
            yb = Tile(sb_root, yb_raw.ap.bitcast(F32), 0, P, yb_raw.b0, 16 * P, 4)
            ysq = alloc(16 * P, BF16)
            sets = [(lth, ltl, eseg, Mh),
                    (ysq[:, 0:8 * P], ysq[:, 8 * P:16 * P], yb_raw[:, 16 * P:24 * P], yb_raw[:, 24 * P:32 * P])]
            if getattr(cfg, "noset1", False):
                sets[1] = sets[0]
            rs = alloc(4 * P)
            sz4 = szT[:, 0:16 * T].ap.rearrange("p (g c t) -> p g c t", g=4, c=4)
            sz3 = szT[:, 0:16 * T].r3(16)

            def cons_z(j, ps):
                K.act(szT[:, j * T:(j + 1) * T], ps, AF.Silu)
            fm_proj(li, "in", CH_MZ, 16, 8, hT, T, cons_z)

            def cons_xbc(j, ps):
                K.copy(("dve", "act")[j % 2], xbcT[:, j * XS + 4:j * XS + 4 + T], ps)
            fm_proj(li, "in", CH_XBC, 24, 8, hT, T, cons_xbc)

            for which, dstT in ((0, BT), (1, CT)):
                for g in range(4):
                    c = 16 + which * 4 + g
                    ps = PS(1)[:, 0:T]
                    for k in range(4):
                        K.mm(ps, diag[:, (c * 4 + k) * P:(c * 4 + k + 1) * P],
                             xbcT[:, c * XS + 1 + k:c * XS + 1 + k + T], start=(k == 0), stop=(k == 3))
                    K.act(dstT[:, g * T:(g + 1) * T], ps, AF.Silu, bias=pcol[:, PC_CONVB + c:PC_CONVB + c + 1])

            wdt = wget("in", li, CH_DT, 1, 8)
            set_ps_pool(4, 8)
            X4 = PSX(0, 4)
            for i in range(nt):
                o = i * P
                tl = tile0 + i
                sp_ = PS(1)
                dps = sp_[:, 0:32]
                if not K.plan:
                    for k in range(8):
                        K.mm(dps, hT[:, k * T + o:k * T + o + P], wdt[:, k * P:k * P + 32], start=(k == 0),
                             stop=(k == 7))
                K.tt("dve", dtv, dps, bc[:, 0:32], ALU.add)
                K.act(ev_, dtv, AF.Exp)
                K.act(dtv, ev_, AF.Ln, bias=b_one)
                if tl == 0:
                    K.ts("dve", dtv, dtv, valid0, ALU.mult)
                K.tt("dve", av, dtv, A_bc, ALU.mult)
                K.copy("dve", ahi, av)
                K.tt("dve", alo, av, ahi, ALU.subtract)
                for q in range(5):
                    cps = PS(1)
                    for c4 in range(4):
                        c = q * 4 + c4
                        oc = cps[:, c4 * P:(c4 + 1) * P]
                        for k in range(4):
                            K.mm(oc, xbcT[:, c * XS + 1 + k + o:c * XS + 1 + k + o + P],
                                 diag[:, (c * 4 + k) * P:(c * 4 + k + 1) * P], start=(k == 0), stop=False)
                        K.mm(oc, ones_b[0:1, :], cbrow_b[0:1, c * P:(c + 1) * P], start=False, stop=True)
                    if q < 4:
                        K.act(xs_tm[:, q * 512:(q + 1) * 512], cps, AF.Silu)
                    else:
                        K.act(B_tm, cps, AF.Silu)
                cbp = PS(1)
                for g in range(4):
                    K.mm(cbp[:, g * P:(g + 1) * P], BT[:, g * T + o:g * T + o + P], CT[:, g * T + o:g * T + o + P])
                K.tt("dve", cbm, cbp, triu_b, ALU.mult, out_ap=cbm.r3(4), in0_ap=cbp.r3(4),
                     in1_ap=triu_b.ap.unsqueeze(1).broadcast_to([P, 4, P]))
                sp2 = PS(1)
                K.mm(sp2[:, 32:64], ones_b, ahi, start=True, stop=False)
                K.mm(sp2[:, 32:64], ones_b, alo, start=False, stop=True)
                K.mm(sp2[:, 64:96], sl_b, ahi, start=True, stop=False)
                K.mm(sp2[:, 64:96], sl_b, alo, start=False, stop=True)
                K.act(cdv, sp2[:, 32:64], AF.Exp)
                K.act(dendv, sp2[:, 64:96], AF.Exp)
                K.tt("dve", dtdv, dtv, dendv, ALU.mult)
                acs = X4
                for h in range(32):
                    c, half = h // 2, h % 2
                    oc = acs[half * 64:(half + 1) * 64, c * P:(c + 1) * P]
                    K.mm(oc, ahi[:, h:h + 1], triu_b, start=True, stop=False,
                         lhsT_ap=ahi[:, h:h + 1].ap.broadcast_to([P, 64]))
                    K.mm(oc, alo[:, h:h + 1], triu_b, start=False, stop=True,
                         lhsT_ap=alo[:, h:h + 1].ap.broadcast_to([P, 64]))
                K.act(ET, acs, AF.Exp)
                K.tt("dve", xdt, xs_tm, dtv, ALU.mult, out_ap=xdt.r3(32), in0_ap=xs_tm.r3(32),
                     in1_ap=dtv.ap.unsqueeze(2).broadcast_to([P, 32, 64]))
                K.tt("dve", xdtd, xs_tm, dtdv, ALU.mult, out_ap=xdtd.r3(32), in0_ap=xs_tm.r3(32),
                     in1_ap=dtdv.ap.unsqueeze(2).broadcast_to([P, 32, 64]))
                A_ps = X4
                segs = {}

                def lt_ops(g):
                    lth_, ltl_ = sets[g % 2][0], sets[g % 2][1]
                    for lt_, a_ in ((lth_, ahi), (ltl_, alo)):
                        K.tt("dve", lt_, sl_b, a_[:, 8 * g:8 * g + 8], ALU.mult, out_ap=lt_.r3(8),
                             in0_ap=sl_b.ap.unsqueeze(1).broadcast_to([P, 8, P]),
                             in1_ap=a_[:, 8 * g:8 * g + 8].ap.unsqueeze(2).broadcast_to([P, 8, P]))

                def seg_mm(g):
                    lth_, ltl_ = sets[g % 2][0], sets[g % 2][1]
                    seg = PSX(4 + 2 * (g % 2), 2)
                    segs[g] = seg
                    for hh in range(8):
                        K.mm(seg[:, hh * P:(hh + 1) * P], lth_[:, hh * P:(hh + 1) * P], triu_b, start=True, stop=False)
                        K.mm(seg[:, hh * P:(hh + 1) * P], ltl_[:, hh * P:(hh + 1) * P], triu_b, start=False, stop=True)

                def exp_mh(g):
                    eseg_, Mh_ = sets[g % 2][2], sets[g % 2][3]
                    K.act(eseg_, segs[g], AF.Exp)
                    K.tt("dve", Mh_, eseg_, cbm[:, g * P:(g + 1) * P], ALU.mult, out_ap=Mh_.r3(8),
                         in0_ap=eseg_.r3(8), in1_ap=cbm[:, g * P:(g + 1) * P].ap.unsqueeze(1).broadcast_to([P, 8, P]))

                def a_mm(g):
                    Mh_ = sets[g % 2][3]
                    for hh in range(8):
                        h = 8 * g + hh
                        c, half = h // 2, h % 2
                        oc = A_ps[half * 64:(half + 1) * 64, c * P:(c + 1) * P]
                        K.mm(oc, xdt[:, h * 64:(h + 1) * 64], Mh_[:, hh * P:(hh + 1) * P], start=True, stop=False)
                        K.mm(oc, xs_tm[:, h * 64:(h + 1) * 64], Dmat[:, h * P:(h + 1) * P], start=False, stop=True)

                lt_ops(0)
                seg_mm(0)
                lt_ops(1)
                seg_mm(1)
                exp_mh(0)
                for g in range(4):
                    a_mm(g)
                    if g + 2 < 4:
                        lt_ops(g + 2)
                        seg_mm(g + 2)
                    if g + 1 < 4:
                        exp_mh(g + 1)
                Bq = PSX(4, 4)
                for c in range(16):
                    g = c // 4
                    K.mm(Bq[:, c * P:(c + 1) * P], ST_b[:, c * P:(c + 1) * P], CT[:, g * T + o:g * T + o + P])
                K.tt("dve", yb, Bq, ET, ALU.mult)
                K.tt("dve", yb, A_ps, yb, ALU.add)
                stp = X4
                for g in range(4):
                    K.mm(stp[:, g * 512:(g + 1) * 512], B_tm[:, g * P:(g + 1) * P], xdtd[:, g * 512:(g + 1) * 512])
                K.tt("dve", ST_f, ST_f, cdv, ALU.mult, out_ap=ST_f.r3(32), in0_ap=ST_f.r3(32),
                     in1_ap=cdv.ap.unsqueeze(2).broadcast_to([P, 32, 64]))
                K.tt("dve", ST_f, stp, ST_f, ALU.add)
                K.copy("act", ST_b, ST_f)
                K.tt("dve", yb, yb, szT, ALU.mult, out_ap=yb.r3(16), in0_ap=yb.r3(16), in1_ap=sz3[:, :, o:o + P])
                K.act(ysq, yb, AF.Square)
                ssp = PS(1)
                for g in range(4):
                    for c4 in range(4):
                        K.mm(ssp[:, g * P:(g + 1) * P], ones_b, ysq[:, (4 * g + c4) * P:(4 * g + c4 + 1) * P],
                             start=(c4 == 0), stop=(c4 == 3))
                K.act(rs, ssp, AF.Ln, bias=b_eps, scale=1.0 / 512)
                K.act(rs, rs, AF.Exp, scale=-0.5)
                K.tt("dve", szT, yb, rs, ALU.mult, out_ap=sz4[:, :, :, o:o + P], in0_ap=yb.r4(4, 4),
                     in1_ap=rs.r3(4).unsqueeze(2).broadcast_to([P, 4, 4, P]))
                if cfg.debug and gi == 1 and li == 0 and i == 0:
                    dbg_dump(yb[:, 0:512], 512)
                    dbg_dump(xs_tm[:, 0:512], 512)
                    dbg_dump(B_tm, 512)
                    dbg_dump(sm[:, 0:192], 192)
                    dbg_dump(ET[:, 0:512], 512)
                    dbg_dump(cbm, 512)
                    dbg_dump(Mh[:, 0:512], 512)
                    dbg_dump(CT[:, 0:128], 128)
                    dbg_dump(BT[:, 0:128], 128)
                    dbg_dump(szT[:, 0:128], 128)
            K.copy("dve", xbcT, xbcT, out_ap=xbcT.r3(24)[:, :, 1:4], in_ap=xbcT.r3(24)[:, :, T + 1:T + 4])
            set_ps_pool(0, 8)
            set_regions([UC])
            branch_merge(li, "bm", 16, szT, 16, T, True)

        def gla_phase(li, gi, tile0, nt, T):
            set_regions([UA, UC])
            glrh = alloc(TMAX, BF16, parts=16)
            glrl = alloc(TMAX, BF16, parts=16)
            sph = alloc(512, BF16)
            spl = alloc(512, BF16)
            tb = alloc(1024)
            sp = tb[:, 0:512]
            EgT = alloc(4 * TMAX)
            EiT = alloc(4 * TMAX)
            qdT = alloc(4 * TMAX, BF16)
            kiT = alloc(4 * TMAX, BF16)
            sr = alloc(8 * TMAX, BF16)
            v_all = alloc(4 * 1024, BF16)
            attnT = alloc(512, BF16)
            kitm = alloc(512, BF16)
            osq = alloc(1024, BF16)
            rs = alloc(512)
            sr3 = sr[:, 0:8 * T].r3(8)
            wl = wget("in", li, CH_GLR, 1, 8)
            gps = PS(1)[0:16, 0:T]
            if not K.plan:
                for k in range(8):
                    K.mm(gps, wl[:, k * P:k * P + 16], hT[:, k * T:(k + 1) * T], start=(k == 0), stop=(k == 7))
            K.copy("act", glrh[:, 0:T], gps)
            K.tt("dve", glrl[:, 0:T], gps, glrh[:, 0:T], ALU.subtract)
            for i in range(nt):
                o = i * P
                zps = PS(1)
                K.mm(zps, glrh[:, o:o + P], gph[0:16, 0:512], start=True, stop=False)
                K.mm(zps, glrl[:, o:o + P], gph[0:16, 0:512], start=False, stop=False)
                K.mm(zps, glrh[:, o:o + P], gpl[0:16, 0:512], start=False, stop=False)
                K.mm(zps, ones_b[0:1, :], gph[0:1, 512:1024], start=False, stop=False)
                K.mm(zps, ones_b[0:1, :], gpl[0:1, 512:1024], start=False, stop=True)
                K.act(sp, zps, AF.Exp, scale=-1.0)
                K.act(sp, sp, AF.Ln, bias=b_one)
                K.copy("act", sph, sp)
                K.tt("dve", spl, sp, sph, ALU.subtract)
                cps = PS(1)
                for h in range(4):
                    K.mm(cps[:, h * P:(h + 1) * P], sph[:, h * P:(h + 1) * P], triu_b, start=True, stop=False)
                    K.mm(cps[:, h * P:(h + 1) * P], spl[:, h * P:(h + 1) * P], triu_b, start=False, stop=True)
                K.act(EgT, cps, AF.Exp, scale=-1.0 / 16, bias=b_lnq,
                      out_ap=EgT[:, 0:4 * T].r3(4)[:, :, o:o + P], in_ap=cps.r3(4))
                K.act(EiT, cps, AF.Exp, scale=1.0 / 16,
                      out_ap=EiT[:, 0:4 * T].r3(4)[:, :, o:o + P], in_ap=cps.r3(4))

            def cons_q(j, ps):
                K.tt("dve", qdT[:, j * T:(j + 1) * T], ps, EgT[:, j * T:(j + 1) * T], ALU.mult)
            fm_proj(li, "in", CH_GQ, 4, 8, hT, T, cons_q)

            def cons_k(j, ps):
                K.tt("dve", kiT[:, j * T:(j + 1) * T], ps, EiT[:, j * T:(j + 1) * T], ALU.mult)
            fm_proj(li, "in", CH_GK, 4, 8, hT, T, cons_k)

            def cons_r(j, ps):
                K.act(sr[:, j * T:(j + 1) * T], ps, AF.Silu)
            fm_proj(li, "in", CH_GR, 8, 8, hT, T, cons_r)

            for half in range(2):
                wv = wget("in", li, CH_GV + 4 * half, 4, 8, tm=True)
                for i in range(nt):
                    o = i * P
                    vps = PS(1)
                    if not K.plan:
                        for k in range(8):
                            K.mm(vps, hT[:, k * T + o:k * T + o + P], wv[:, k * 512:(k + 1) * 512], start=(k == 0),
                                 stop=(k == 7))
                    K.copy(("act", "dve")[i % 2], v_all[:, i * 1024 + half * 512:i * 1024 + (half + 1) * 512], vps)

            for i in range(nt):
                o = i * P
                v_tm = v_all[:, i * 1024:(i + 1) * 1024]
                aps = PSX(0, 1)
                kps = PSX(1, 1)
                for h in range(4):
                    K.mm(aps[:, h * P:(h + 1) * P], kiT[:, h * T + o:h * T + o + P], qdT[:, h * T + o:h * T + o + P])
                for h in range(4):
                    K.mm(kps[:, h * P:(h + 1) * P], kiT[:, h * T + o:h * T + o + P], ident_b)
                K.tt("dve", attnT, aps, triu_b, ALU.mult, out_ap=attnT.r3(4), in0_ap=aps.r3(4),
                     in1_ap=triu_b.ap.unsqueeze(1).broadcast_to([P, 4, P]))
                K.copy("act", kitm, kps)
                kvp = PSX(4, 2)
                for h in range(4):
                    K.mm(kvp[:, h * 256:(h + 1) * 256], kitm[:, h * P:(h + 1) * P], v_tm[:, h * 256:(h + 1) * 256])
                K.tt("dve", S_f, kvp, S_f, ALU.add)
                for h in range(4):
                    dcol = EgT[:, h * T + o + P - 1:h * T + o + P]
                    K.ts("dve", S_f[:, h * 256:(h + 1) * 256], S_f[:, h * 256:(h + 1) * 256], dcol,
                         ALU.mult, 1.0 / QSCALE, ALU.mult)
                ops_ = PSX(2, 2)
                for h in range(4):
                    for vc in range(2):
                        c = 2 * h + vc
                        oc = ops_[:, c * P:(c + 1) * P]
                        K.mm(oc, v_tm[:, c * P:(c + 1) * P], attnT[:, h * P:(h + 1) * P], start=True, stop=False)
                        K.mm(oc, S_b[:, h * 256 + vc * P:h * 256 + (vc + 1) * P], qdT[:, h * T + o:h * T + o + P],
                             start=False, stop=True)
                K.copy("act", S_b, S_f)
                K.act(osq, ops_, AF.Square)
                ssp = PSX(6, 1)
                for h in range(4):
                    for vc in range(2):
                        K.mm(ssp[:, h * P:(h + 1) * P], ones_b, osq[:, (2 * h + vc) * P:(2 * h + vc + 1) * P],
                             start=(vc == 0), stop=(vc == 1))
                K.act(rs, ssp, AF.Ln, bias=b_eps, scale=1.0 / 256)
                K.act(rs, rs, AF.Exp, scale=-0.5)
                K.tt("dve", tb, ops_, rs, ALU.mult, out_ap=tb.r4(4, 2), in0_ap=ops_.r4(4, 2),
                     in1_ap=rs.r3(4).unsqueeze(2).broadcast_to([P, 4, 2, P]))
                K.tt("dve", sr, tb, sr, ALU.mult, out_ap=sr3[:, :, o:o + P], in0_ap=tb.r3(8),
                     in1_ap=sr3[:, :, o:o + P])
            branch_merge(li, "bg", 0, sr, 8, T, False)

        def swa_phase(li, gi, tile0, nt, T):
            set_regions([UA, UC])
            sqT = alloc(8 * TMAX, BF16)
            pT = [alloc(512, BF16) for _ in range(8)]
            dn = alloc(1024)
            swaT = alloc(8 * TMAX, BF16)

            sq4 = sqT[:, 0:8 * T].ap.rearrange("p (i c t) -> p i c t", i=nt, c=8)

            def cons_q(j, ps):
                K.copy(("act", "dve")[j % 2], sqT, ps, out_ap=sq4[:, :, j, :],
                       in_ap=ps.ap.rearrange("p (i t) -> p i t", i=nt))
            fm_proj(li, "in", CH_SQ, 8, 8, hT, T, cons_q)

            def cons_k(j, ps):
                dst = (kTa_h, kTb_h)[j]
                K.copy("dve", dst[:, P:P + T], ps)
            fm_proj(li, "in", CH_SKA, 2, 8, hT, T, cons_k)
            wsv = wget("in", li, CH_SV, 1, 8)
            sw3 = swaT[:, 0:8 * T].r3(8)
            set_ps_pool(0, 4)
            for i in range(nt):
                o = i * P
                tl = tile0 + i
                vps = PS(1)[:, 0:P]
                if not K.plan:
                    for k in range(8):
                        K.mm(vps, hT[:, k * T + o:k * T + o + P], wsv[:, k * P:(k + 1) * P], start=(k == 0),
                             stop=(k == 7))
                vs = vh[:, (i + 1) * 512:(i + 2) * 512]
                K.copy("dve", vs[:, 0:64], vps[:, 0:64])
                K.copy("dve", vs[:, 128 + 64:256], vps[:, 0:64])
                K.copy("dve", vs[:, 256:256 + 64], vps[:, 64:128])
                K.copy("dve", vs[:, 384 + 64:512], vps[:, 64:128])
                vp = vh[:, i * 512:(i + 1) * 512]
                kbs = ([] if tl == 0 else [0]) + [1]
                for gq in range(4):
                    kv, par = gq // 2, gq % 2
                    kh = (kTa_h, kTb_h)[kv]
                    pr = slice(par * 64, par * 64 + 64)
                    for kb in kbs:
                        sps = PS(1)
                        kcol = o + kb * P
                        qoff = i * 1024 + kv * 512
                        K.mm(sps, kh[pr, kcol:kcol + P], sqT[pr, qoff:qoff + 512], start=True, stop=False)
                        if kb == 1:
                            mk = mC0 if tl == 0 else mC
                        else:
                            mk = mP1 if tl == 1 else mP
                        K.mm(sps, ident_b, mk, start=False, stop=True)
                        K.act(pT[gq * 2 + kb], sps, AF.Exp, scale=0.125)
                ops_ = PSX(4, 2)
                dps = PSX(6, 2)
                for c in range(8):
                    kv = c // 4
                    seq = [(par, kb) for par in range(2) for kb in kbs]
                    for n, (par, kb) in enumerate(seq):
                        gq = kv * 2 + par
                        pt = pT[gq * 2 + kb][:, (c % 4) * P:(c % 4 + 1) * P]
                        vsrc = vs if kb == 1 else vp
                        vv = vsrc[:, (kv * 2 + par) * P:(kv * 2 + par + 1) * P]
                        K.mm(ops_[:, c * P:(c + 1) * P], vv, pt, start=(n == 0), stop=(n == len(seq) - 1))
                    for n, (par, kb) in enumerate(seq):
                        gq = kv * 2 + par
                        pt = pT[gq * 2 + kb][:, (c % 4) * P:(c % 4 + 1) * P]
                        K.mm(dps[:, c * P:(c + 1) * P], (onesA, onesB)[par], pt, start=(n == 0),
                             stop=(n == len(seq) - 1))
                K.tt("dve", dn, dps, esinkE, ALU.add, out_ap=dn.r3(8), in0_ap=dps.r3(8),
                     in1_ap=esinkE.ap.unsqueeze(2).broadcast_to([P, 8, P]))
                K.act(dn, dn, AF.Ln)
                K.act(dn, dn, AF.Exp, scale=-1.0)
                K.tt("dve", swaT, ops_, dn, ALU.mult, out_ap=sw3[:, :, o:o + P], in0_ap=ops_.r3(8), in1_ap=dn.r3(8))
            set_ps_pool(0, 8)
            K.copy("dve", kTa_h[:, 0:P], kTa_h[:, T:T + P])
            K.copy("dve", kTb_h[:, 0:P], kTb_h[:, T:T + P])
            K.copy("act", vh[:, 0:512], vh[:, nt * 512:(nt + 1) * 512])
            branch_merge(li, "bs", 8, swaT, 8, T, False)

        K.plan = True
        body()
        K.plan = False
        body()

        for i in range(getattr(cfg, "dummy_sems", 0)):
            es.enter_context(nc.semaphore(f"dummy{i}"))
        for ev in K.evs.values():
            ev.sem = es.enter_context(nc.semaphore(ev.name))
        print("[kernel] semaphores:", {ev.name: getattr(ev.sem, "num", ev.sem) for ev in K.evs.values()})
        block = es.enter_context(nc.Block())

        def replay(name):
            def run(e):
                for fn, waits, inc in K.eng[name].ops:
                    if fn is None:
                        for ev, val in waits:
                            e.wait_ge(ev.sem, val)
                        continue
                    attach = None
                    ws = list(waits)
                    if ws and name in ("dve", "pool") and not (inc is not None and inc[0].kind == "dma"):
                        attach = ws.pop()
                    for ev, val in ws:
                        e.wait_ge(ev.sem, val)
                    ins = fn(e)
                    if attach is not None:
                        ins._wait_ge(attach[0].sem, attach[1])
                    if inc is not None:
                        ins.then_inc(inc[0].sem, inc[1])
            return run

        block.sync(replay("sp"))
        block.tensor(replay("pe"))
        block.scalar(replay("act"))
        block.vector(replay("dve"))
        block.gpsimd(replay("pool"))
        stats = {n: len(K.eng[n].ops) for n in K.eng}
        print("[kernel] ops per engine:", stats)
    return nc, stats


def _col(v, kc):
    return np.ascontiguousarray(np.asarray(v, np.float32).reshape(kc, P).T)


def _layer_params(inp, l):
    pc = np.zeros((P, NPC), np.float32)
    pc[:, PC_BGATE:PC_BGATE + 24] = _col(inp["b_gate"][l], 24)
    pc[:, PC_CONVB:PC_CONVB + 24] = _col(inp["ssd_conv_b"][l], 24)
    cw = np.asarray(inp["ssd_conv_w"][l], np.float32)
    pc[:, PC_CONVW:PC_CONVW + 96] = cw.reshape(4, 24, P).transpose(2, 1, 0).reshape(P, 96)
    sk = np.asarray(inp["swa_sinks"][l], np.float32)
    pc[:, PC_SINK:PC_SINK + 8] = sk.reshape(8, 2)[:, (np.arange(P) >= 64).astype(np.int64)].T
    pc[:, PC_GFIN:PC_GFIN + 8] = _col(inp["final_norm"], 8)
    pc[:, PC_GMIX:PC_GMIX + 8] = _col(inp["norm_mix"][l], 8)
    pc[:, PC_GMLP:PC_GMLP + 8] = _col(inp["norm_mlp"][l], 8)
    pc[:, PC_GGLA:PC_GGLA + 8] = _col(np.tile(np.asarray(inp["gla_norm"][l], np.float32), 4), 8)
    pc[:, PC_GSSD:PC_GSSD + 16] = _col(inp["ssd_norm"][l], 16)
    gpa = np.zeros((16, 1024), np.float32)
    gpa[:, 0:512] = inp["gla_w_gate_up"][l]
    gpa[0, 512:1024] = inp["gla_b_gate_up"][l]
    cbrow = np.asarray(inp["ssd_conv_b"][l], np.float32)[None, 0:2560]
    bcr = np.concatenate([inp["ssd_dt_bias"][l], inp["ssd_A_log"][l], inp["ssd_D"][l]]).astype(np.float32)[None]
    return pc, gpa, np.ascontiguousarray(cbrow), bcr


def _lead_tile(meta_tokens):
    lead = np.zeros((P, D), np.float32)
    lead[PADN:] = np.asarray(meta_tokens, np.float32)
    return lead


_PROG_CACHE = {}


def _get_prog(key, cfg):
    if key not in _PROG_CACHE:
        _PROG_CACHE[key] = build_program(cfg)[0]
    return _PROG_CACHE[key]


def _weights_for(inp, layers):
    perm = _w_in_perm()
    w_in = np.asarray(inp["w_in"], np.float32)
    outs = []
    for l in layers:
        wl = np.zeros((D, NCH_IN * P), np.float32)
        m = perm >= 0
        wl[:, m] = w_in[l][:, perm[m]]
        outs.append(wl)
    d = {"w_in": np.stack(outs)}
    for nm, key in (("w_bg", "w_branch_gla"), ("w_bs", "w_branch_swa"), ("w_bm", "w_branch_ssd"),
                    ("w_out", "w_out"), ("w_m1", "w_mlp_in"), ("w_m2", "w_mlp_out")):
        d[nm] = np.ascontiguousarray(np.asarray(inp[key], np.float32)[list(layers)])
    ps = [_layer_params(inp, l) for l in layers]
    d["pcol"] = np.stack([p[0] for p in ps])
    d["gp"] = np.stack([p[1] for p in ps])
    d["cbrow"] = np.stack([p[2] for p in ps])
    d["bc"] = np.stack([p[3] for p in ps])
    d["consts"] = _host_consts()
    return d


FUSED = True


def kernel(**inputs):
    x = np.asarray(inputs["x"], np.float32)
    B = x.shape[0]
    meta = _lead_tile(inputs["meta_tokens"])
    if FUSED:
        cfg = Cfg(list(range(DEPTH)), True, True)
        nc = _get_prog("fused", cfg)
        shared = _weights_for(inputs, range(DEPTH))
        in_maps = [dict(shared, x=np.ascontiguousarray(x[b]), meta=meta) for b in range(B)]
        res = run_bass_kernel_spmd(nc, in_maps, core_ids=list(range(B)))
        return np.stack([r["out"] for r in res.results]).astype(np.float32)
    xres = None
    for l in range(DEPTH):
        first, last = l == 0, l == DEPTH - 1
        key = "first" if first else ("last" if last else "mid")
        nc = _get_prog(key, Cfg([l], first, last))
        shared = _weights_for(inputs, [l])
        in_maps = []
        for b in range(B):
            m = dict(shared)
            if first:
                m["x"] = np.ascontiguousarray(x[b])
                m["meta"] = meta
            else:
                m["xres_in"] = xres[b]
            in_maps.append(m)
        res = run_bass_kernel_spmd(nc, in_maps, core_ids=list(range(B)))
        if last:
            return np.stack([r["out"] for r in res.results]).astype(np.float32)
        xres = [np.ascontiguousarray(r["xres_out"]) for r in res.results]
```
